# Optimizing a Trainium2 kernel written in Bass

```python
import jax, jax.numpy as jnp
from jax import lax
import numpy as np

D_MODEL = 2048
BATCH = 4
SEQ = 2048
DEPTH = 1
DEC_BATCH = 128
DEC_SEQ = 4
PAST_LEN = 16384
PAGE_SIZE = 128

N_META = 16
CONV_W = 4
D_LRU = D_MODEL
LRU_HEADS = 16
LRU_HEAD_DIM = D_LRU // LRU_HEADS
LRU_C = 8.0
D_SSD = D_MODEL
SSD_HEAD_DIM = 64
SSD_HEADS = D_SSD // SSD_HEAD_DIM
SSD_GROUPS = 4
SSD_STATE = 128
SSD_CHUNK = 128
D_BC = SSD_GROUPS * SSD_STATE
D_CONV_SSD = D_SSD + 2 * D_BC
D_MIX = D_LRU + D_SSD
D_IN_PROJ = 2 * D_LRU + D_SSD + D_CONV_SSD + SSD_HEADS
D_FF = 4 * D_MODEL
EPS = 1e-6

kernel_name = "hymba_rglru_ssd_decoder_step"


def rmsnorm(x, g):
    xf = x.astype(jnp.float32)
    y = xf * lax.rsqrt(jnp.mean(xf * xf, axis=-1, keepdims=True) + EPS)
    return (y * g.astype(jnp.float32)).astype(x.dtype)


def causal_conv(u, w, b, buf):
    L = u.shape[1]
    ext = jnp.concatenate([buf.astype(u.dtype), u], axis=1)
    out = b + ext[:, 0:L] * w[0]
    for k in range(1, CONV_W):
        out = out + ext[:, k:k + L] * w[k]
    return out, ext[:, L:]


def _lin_combine(e1, e2):
    a1, b1 = e1
    a2, b2 = e2
    return a1 * a2, a2 * b1 + b2


def rglru(xc, w_a, b_a, w_x, b_x, lam, h0, reset_first):
    Bn, L, _ = xc.shape
    xf = xc.astype(jnp.float32)
    xh = xf.reshape(Bn, L, LRU_HEADS, LRU_HEAD_DIM)
    r = jax.nn.sigmoid(jnp.einsum('blhi,hij->blhj', xh, w_a.astype(jnp.float32)) + b_a.astype(jnp.float32)).reshape(Bn, L, D_LRU)
    i = jax.nn.sigmoid(jnp.einsum('blhi,hij->blhj', xh, w_x.astype(jnp.float32)) + b_x.astype(jnp.float32)).reshape(Bn, L, D_LRU)
    log_a = -LRU_C * r * jax.nn.softplus(-lam.astype(jnp.float32))
    a = jnp.exp(log_a)
    mult = jnp.sqrt(-jnp.expm1(2.0 * log_a))
    if reset_first:
        mult = mult.at[:, 0].set(1.0)
    bt = mult * i * xf
    bt = bt.at[:, 0].add(a[:, 0] * h0.astype(jnp.float32))
    _, h = lax.associative_scan(_lin_combine, (a, bt), axis=1)
    return h, h[:, -1]


def ssd_chunked(x, dt, A, Bm, Cm, h0, chunk):
    b, L, H, P = x.shape
    G, N = Bm.shape[2], Bm.shape[3]
    R = H // G
    nc = L // chunk
    xr = x.reshape(b, nc, chunk, G, R, P)
    dtc = dt.reshape(b, nc, chunk, H)
    Bc = Bm.reshape(b, nc, chunk, G, N)
    Cc = Cm.reshape(b, nc, chunk, G, N)
    acum = jnp.cumsum(dtc * A, axis=2)
    seg = acum[:, :, :, None, :] - acum[:, :, None, :, :]
    causal = jnp.tril(jnp.ones((chunk, chunk), dtype=bool))
    lmat = jnp.exp(jnp.where(causal[None, None, :, :, None], seg, -jnp.inf))
    lmat = (lmat * dtc[:, :, None, :, :]).reshape(b, nc, chunk, chunk, G, R)
    cb = jnp.einsum('bcign,bcjgn->bcgij', Cc, Bc)
    m = jnp.einsum('bcgij,bcijgr->bcijgr', cb, lmat)
    y_diag = jnp.einsum('bcijgr,bcjgrp->bcigrp', m, xr)
    w_end = (jnp.exp(acum[:, :, -1:, :] - acum) * dtc).reshape(b, nc, chunk, G, R)
    states = jnp.einsum('bcjgn,bcjgrp->bcgrpn', Bc, xr * w_end[..., None]).reshape(b, nc, H, P, N)
    chunk_decay = jnp.exp(acum[:, :, -1, :])

    def step(h, inp):
        st, dec = inp
        return dec[:, :, None, None] * h + st, h

    h_final, h_starts = lax.scan(step, h0.astype(jnp.float32),
                                 (jnp.swapaxes(states, 0, 1), jnp.swapaxes(chunk_decay, 0, 1)))
    h_starts = jnp.swapaxes(h_starts, 0, 1).reshape(b, nc, G, R, P, N)
    y_off = jnp.einsum('bcign,bcgrpn->bcigrp', Cc, h_starts) * jnp.exp(acum).reshape(b, nc, chunk, G, R)[..., None]
    return (y_diag + y_off).reshape(b, L, H, P), h_final


def mixer(h, lru_buf, lru_h0, ssd_buf, ssd_h0, w_in, conv_lru_w, conv_lru_b, lru_wa, lru_ba, lru_wx, lru_bx,
          lru_lambda, g_lru_out, conv_ssd_w, conv_ssd_b, dt_bias, a_log, d_skip, g_ssd_out, w_out,
          reset_first, pad, chunk):
    Bn, L, _ = h.shape
    proj = h @ w_in
    o1 = D_LRU
    o2 = o1 + D_LRU
    o3 = o2 + D_SSD
    o4 = o3 + D_CONV_SSD
    gate_lru, x_lru, z, xbc_raw, dt_raw = proj[..., :o1], proj[..., o1:o2], proj[..., o2:o3], proj[..., o3:o4], proj[..., o4:]
    xl, lru_buf_new = causal_conv(x_lru, conv_lru_w, conv_lru_b, lru_buf)
    hl, lru_h_new = rglru(xl, lru_wa, lru_ba, lru_wx, lru_bx, lru_lambda, lru_h0, reset_first)
    lru_out = rmsnorm(hl * jax.nn.gelu(gate_lru.astype(jnp.float32)), g_lru_out)
    xbc, ssd_buf_new = causal_conv(xbc_raw, conv_ssd_w, conv_ssd_b, ssd_buf)
    xbc = jax.nn.silu(xbc.astype(jnp.float32))
    xs = xbc[..., :D_SSD].reshape(Bn, L, SSD_HEADS, SSD_HEAD_DIM)
    Bm = xbc[..., D_SSD:D_SSD + D_BC].reshape(Bn, L, SSD_GROUPS, SSD_STATE)
    Cm = xbc[..., D_SSD + D_BC:].reshape(Bn, L, SSD_GROUPS, SSD_STATE)
    dt = jax.nn.softplus(dt_raw.astype(jnp.float32) + dt_bias.astype(jnp.float32))
    A = -jnp.exp(a_log.astype(jnp.float32))
    if pad > 0:
        xs_p = jnp.pad(xs, ((0, 0), (pad, 0), (0, 0), (0, 0)))
        dt_p = jnp.pad(dt, ((0, 0), (pad, 0), (0, 0)))
        B_p = jnp.pad(Bm, ((0, 0), (pad, 0), (0, 0), (0, 0)))
        C_p = jnp.pad(Cm, ((0, 0), (pad, 0), (0, 0), (0, 0)))
    else:
        xs_p, dt_p, B_p, C_p = xs, dt, Bm, Cm
    y, ssd_h_new = ssd_chunked(xs_p, dt_p, A, B_p, C_p, ssd_h0, chunk)
    y = y[:, pad:] + d_skip.astype(jnp.float32)[:, None] * xs
    yg = y.reshape(Bn, L, D_SSD) * jax.nn.silu(z.astype(jnp.float32))
    yg = yg.reshape(Bn, L, SSD_GROUPS, D_SSD // SSD_GROUPS)
    yg = yg * lax.rsqrt(jnp.mean(yg * yg, axis=-1, keepdims=True) + EPS)
    ssd_out = (yg.reshape(Bn, L, D_SSD) * g_ssd_out.astype(jnp.float32)).astype(h.dtype)
    mix = jnp.concatenate([lru_out.astype(h.dtype), ssd_out], axis=-1) @ w_out
    return mix, lru_h_new, lru_buf_new, ssd_h_new, ssd_buf_new


def setup_inputs(seed: int = 0) -> dict:
    key = jax.random.key(seed)
    ks = jax.random.split(key, 32)
    f32 = jnp.float32

    def nrm(k, shape, scale):
        return jax.random.normal(k, shape, f32) * scale

    a0 = jax.random.uniform(ks[10], (DEPTH, D_LRU), f32, 0.9, 0.999)
    s = a0 ** (1.0 / LRU_C)
    lru_lambda = jnp.log(s) - jnp.log1p(-s)
    dt0 = jnp.exp(jax.random.uniform(ks[14], (DEPTH, SSD_HEADS), f32, np.log(1e-3), np.log(1e-1)))
    dt_bias = dt0 + jnp.log(-jnp.expm1(-dt0))
    return {
        "x_prompt": nrm(ks[0], (BATCH, SEQ, D_MODEL), 1.0),
        "x_sample": nrm(ks[1], (DEC_BATCH, DEC_SEQ, D_MODEL), 1.0),
        "state_lru_h": nrm(ks[2], (DEPTH, DEC_BATCH, D_LRU), 0.5),
        "state_lru_conv": nrm(ks[3], (DEPTH, DEC_BATCH, CONV_W - 1, D_LRU), 1.0),
        "state_ssd": nrm(ks[4], (DEPTH, DEC_BATCH, SSD_HEADS, SSD_HEAD_DIM, SSD_STATE), 0.1),
        "state_ssd_conv": nrm(ks[5], (DEPTH, DEC_BATCH, CONV_W - 1, D_CONV_SSD), 1.0),
        "meta_tokens": nrm(ks[6], (N_META, D_MODEL), 1.0),
        "g_mix": 1.0 + nrm(ks[7], (DEPTH, D_MODEL), 0.02),
        "w_in": nrm(ks[8], (DEPTH, D_MODEL, D_IN_PROJ), D_MODEL ** -0.5),
        "conv_lru_w": nrm(ks[9], (DEPTH, CONV_W, D_LRU), CONV_W ** -0.5),
        "conv_lru_b": nrm(ks[11], (DEPTH, D_LRU), 0.02),
        "lru_wa": nrm(ks[12], (DEPTH, LRU_HEADS, LRU_HEAD_DIM, LRU_HEAD_DIM), LRU_HEAD_DIM ** -0.5),
        "lru_ba": nrm(ks[13], (DEPTH, LRU_HEADS, LRU_HEAD_DIM), 0.02),
        "lru_wx": nrm(ks[15], (DEPTH, LRU_HEADS, LRU_HEAD_DIM, LRU_HEAD_DIM), LRU_HEAD_DIM ** -0.5),
        "lru_bx": nrm(ks[16], (DEPTH, LRU_HEADS, LRU_HEAD_DIM), 0.02),
        "lru_lambda": lru_lambda,
        "g_lru_out": 1.0 + nrm(ks[17], (DEPTH, D_LRU), 0.02),
        "conv_ssd_w": nrm(ks[18], (DEPTH, CONV_W, D_CONV_SSD), CONV_W ** -0.5),
        "conv_ssd_b": nrm(ks[19], (DEPTH, D_CONV_SSD), 0.02),
        "dt_bias": dt_bias,
        "a_log": jnp.log(jax.random.uniform(ks[20], (DEPTH, SSD_HEADS), f32, 1.0, 16.0)),
        "d_skip": 1.0 + nrm(ks[21], (DEPTH, SSD_HEADS), 0.1),
        "g_ssd_out": 1.0 + nrm(ks[22], (DEPTH, D_SSD), 0.02),
        "w_out": nrm(ks[23], (DEPTH, D_MIX, D_MODEL), D_MIX ** -0.5),
        "g_mlp": 1.0 + nrm(ks[24], (DEPTH, D_MODEL), 0.02),
        "w_up": nrm(ks[25], (DEPTH, D_MODEL, D_FF), D_MODEL ** -0.5),
        "w_down": nrm(ks[26], (DEPTH, D_FF, D_MODEL), D_FF ** -0.5),
        "g_final": 1.0 + nrm(ks[27], (D_MODEL,), 0.02),
    }


def reference(x_prompt, x_sample, state_lru_h, state_lru_conv, state_ssd, state_ssd_conv, meta_tokens,
              g_mix, w_in, conv_lru_w, conv_lru_b, lru_wa, lru_ba, lru_wx, lru_bx, lru_lambda, g_lru_out,
              conv_ssd_w, conv_ssd_b, dt_bias, a_log, d_skip, g_ssd_out, w_out, g_mlp, w_up, w_down, g_final):
    Bp = x_prompt.shape[0]
    xp = jnp.concatenate([jnp.broadcast_to(meta_tokens.astype(x_prompt.dtype)[None], (Bp, N_META, D_MODEL)), x_prompt], axis=1)
    Lp = xp.shape[1]
    pad_p = (-Lp) % SSD_CHUNK
    xs = x_sample
    p_states = ([], [], [], [])
    s_states = ([], [], [], [])
    for l in range(DEPTH):
        w = (w_in[l], conv_lru_w[l], conv_lru_b[l], lru_wa[l], lru_ba[l], lru_wx[l], lru_bx[l], lru_lambda[l],
             g_lru_out[l], conv_ssd_w[l], conv_ssd_b[l], dt_bias[l], a_log[l], d_skip[l], g_ssd_out[l], w_out[l])
        mix_p, lh, lb, sh, sb = mixer(
            rmsnorm(xp, g_mix[l]),
            jnp.zeros((Bp, CONV_W - 1, D_LRU), xp.dtype), jnp.zeros((Bp, D_LRU), jnp.float32),
            jnp.zeros((Bp, CONV_W - 1, D_CONV_SSD), xp.dtype),
            jnp.zeros((Bp, SSD_HEADS, SSD_HEAD_DIM, SSD_STATE), jnp.float32),
            *w, reset_first=True, pad=pad_p, chunk=SSD_CHUNK)
        for lst, v in zip(p_states, (lh, lb, sh, sb)):
            lst.append(v)
        xp = xp + mix_p
        hp = rmsnorm(xp, g_mlp[l])
        xp = xp + jnp.square(jax.nn.relu(hp @ w_up[l])) @ w_down[l]
        mix_s, lh, lb, sh, sb = mixer(
            rmsnorm(xs, g_mix[l]), state_lru_conv[l], state_lru_h[l], state_ssd_conv[l], state_ssd[l],
            *w, reset_first=False, pad=0, chunk=xs.shape[1])
        for lst, v in zip(s_states, (lh, lb, sh, sb)):
            lst.append(v)
        xs = xs + mix_s
        hs = rmsnorm(xs, g_mlp[l])
        xs = xs + jnp.square(jax.nn.relu(hs @ w_up[l])) @ w_down[l]
    y_prompt = rmsnorm(xp, g_final)[:, N_META:]
    y_sample = rmsnorm(xs, g_final)
    dts = (state_lru_h.dtype, state_lru_conv.dtype, state_ssd.dtype, state_ssd_conv.dtype)
    p_lru_h, p_lru_conv, p_ssd, p_ssd_conv = [jnp.stack(v, axis=0).astype(d) for v, d in zip(p_states, dts)]
    s_lru_h, s_lru_conv, s_ssd, s_ssd_conv = [jnp.stack(v, axis=0).astype(d) for v, d in zip(s_states, dts)]
    return (y_prompt, y_sample, p_lru_h, p_lru_conv, p_ssd, p_ssd_conv, s_lru_h, s_lru_conv, s_ssd, s_ssd_conv)
```

```python
import numpy as np
import concourse.bass as bass
import concourse.mybir as mybir
from concourse.bass_utils import run_bass_kernel_spmd
from contextlib import ExitStack

F32 = mybir.dt.float32
BF16 = mybir.dt.bfloat16
AF = mybir.ActivationFunctionType
ALU = mybir.AluOpType
AX = mybir.AxisListType


class _Op:
    __slots__ = ("eng", "fn", "reads", "writes", "dma", "deps", "waits", "sig", "sem", "semval", "idx", "barrier", "prev_wait")

    def __init__(self, eng, fn, reads, writes, dma):
        self.eng = eng
        self.fn = fn
        self.reads = tuple(reads)
        self.writes = tuple(writes)
        self.dma = dma
        self.waits = []
        self.sig = False
        self.sem = None
        self.semval = 0
        self.barrier = False
        self.prev_wait = None


class Sched:
    ENGS = ("pe", "act", "dve", "pool", "sp")
    SAME_ENG_DIST = 10 ** 9

    def __init__(self, nc, stack, n_dma_sems=None):
        self.nc = nc
        self.ops = []
        self.stack = stack
        nd = n_dma_sems or {"sp": 24, "act": 2, "pool": 12}
        self.csem = {e: stack.enter_context(nc.semaphore("cs_" + e)) for e in ("pe", "act", "dve", "pool")}
        self.dsem = {q: [stack.enter_context(nc.semaphore("ds_%s_%d" % (q, i))) for i in range(n)]
                     for q, n in nd.items()}

    def add(self, eng, fn, reads=(), writes=(), dma=False):
        op = _Op(eng, fn, reads, writes, dma)
        op.idx = len(self.ops)
        self.ops.append(op)
        return op

    def pe(self, fn, r=(), w=()): return self.add("pe", fn, r, w)
    def act(self, fn, r=(), w=()): return self.add("act", fn, r, w)
    def dve(self, fn, r=(), w=()): return self.add("dve", fn, r, w)
    def pool(self, fn, r=(), w=()): return self.add("pool", fn, r, w)
    def dma(self, q, fn, r=(), w=()): return self.add(q, fn, r, w, dma=True)

    def barrier(self):
        for e in self.ENGS:
            op = self.add(e, None)
            op.barrier = True

    def finalize(self):
        ops = self.ops
        last_writer = {}
        readers = {}
        last_real = {}
        dmas_since = []
        for op in ops:
            if op.barrier:
                deps = {}
                for e, i in last_real.items():
                    if e != op.eng:
                        deps[i] = True
                for i in dmas_since:
                    deps[i] = True
                op.deps = deps
                if op.eng == self.ENGS[-1]:
                    dmas_since = []
                continue
            if op.fn is not None:
                if op.dma:
                    dmas_since.append(op.idx)
                else:
                    last_real[op.eng] = op.idx
            deps = {}
            for k in op.reads:
                lw = last_writer.get(k)
                if lw is not None:
                    deps[lw] = True
            for k in op.writes:
                lw = last_writer.get(k)
                if lw is not None and lw not in deps:
                    deps[lw] = deps.get(lw, False)
                for r in readers.get(k, ()):
                    if r not in deps:
                        deps[r] = False
            deps.pop(op.idx, None)
            op.deps = deps
            for k in op.reads:
                lst = readers.setdefault(k, [])
                if not op.dma:
                    lst[:] = [r for r in lst if ops[r].dma or ops[r].eng != op.eng]
                lst.append(op.idx)
            for k in op.writes:
                last_writer[k] = op.idx
                readers[k] = []
        need = []
        pos = {}
        cnt = {e: 0 for e in self.ENGS}
        for op in ops:
            if op.fn is not None and not op.dma:
                cnt[op.eng] += 1
            pos[op.idx] = cnt[op.eng]
        for op in ops:
            lst = []
            for d, raw in op.deps.items():
                dop = ops[d]
                if dop.dma:
                    lst.append(d)
                elif dop.eng == op.eng:
                    if op.dma:
                        lst.append(d)
                    elif op.eng == "pe":
                        continue
                    else:
                        lst.append(d)
                else:
                    lst.append(d)
            need.append(lst)
            for d in lst:
                ops[d].sig = True
        ccount = {e: 0 for e in self.csem}
        dcount = {q: [0] * len(v) for q, v in self.dsem.items()}
        drr = {q: 0 for q in self.dsem}
        for op in ops:
            if op.dma:
                q = op.eng
                i = drr[q] % len(self.dsem[q])
                drr[q] += 1
                op.prev_wait = (self.dsem[q][i], dcount[q][i]) if dcount[q][i] > 0 else None
                dcount[q][i] += 16
                op.sem = self.dsem[q][i]
                op.semval = dcount[q][i]
                op.sig = True
            elif op.sig and op.fn is not None:
                ccount[op.eng] += 1
                op.sem = self.csem[op.eng]
                op.semval = ccount[op.eng]
        waited = {e: {} for e in self.ENGS}
        for op, lst in zip(ops, need):
            w = {}
            pw = getattr(op, "prev_wait", None)
            if pw is not None:
                w[id(pw[0])] = pw
            for d in lst:
                dop = ops[d]
                key = id(dop.sem)
                if w.get(key, (None, 0))[1] < dop.semval:
                    w[key] = (dop.sem, dop.semval)
            wl = []
            wd = waited[op.eng]
            for key, (sem, val) in w.items():
                if wd.get(key, 0) < val:
                    wd[key] = val
                    wl.append((sem, val))
            op.waits = wl
        self.n_ops = len(ops)

    def emit(self):
        nc = self.nc
        self.finalize()
        per = {e: [op for op in self.ops if op.eng == e] for e in self.ENGS}

        def run(eng, lst):
            for op in lst:
                for sem, val in op.waits:
                    eng.wait_ge(sem, val)
                if op.fn is not None:
                    inst = op.fn(eng)
                    if op.sig:
                        inst.then_inc(op.sem, 16 if op.dma else 1)

        with nc.Block() as block:
            @block.sync
            def _(e):
                run(e, per["sp"])

            @block.scalar
            def _(e):
                run(e, per["act"])

            @block.vector
            def _(e):
                run(e, per["dve"])

            @block.gpsimd
            def _(e):
                run(e, per["pool"])

            @block.tensor
            def _(e):
                run(e, per["pe"])


D = 2048
KC = 16
NPRE = 1040
NM = 1092
NU = 1088
O1, O2, O3, O4 = 2048, 4096, 6144, 9216
EPS = 1e-6
NEG = -30000.0

C_GMIX, C_GMLP, C_CLW, C_CLB, C_BA, C_BX, C_LAM, C_GLRU = 0, 16, 32, 96, 112, 128, 144, 160
C_CSW, C_CSB, C_GSSD, C_DSK, C_DTB, C_ALOG, C_FLAG, NCOLS = 176, 272, 296, 312, 328, 329, 330, 336


def build(stage=99):
    nc = bass.Bass("TRN2", target_bir_lowering=False)

    def din(name, shape, dt=F32):
        return nc.dram_tensor(name, list(shape), dt, kind="ExternalInput").ap()

    def dout(name, shape, dt=F32):
        return nc.dram_tensor(name, list(shape), dt, kind="ExternalOutput").ap()

    def dscr(name, shape, dt=F32):
        return nc.dram_tensor(name, list(shape), dt).ap()

    xp = din("xp", [NPRE, D])
    xm = din("xm", [NM, D])
    cols_d = din("cols", [128, NCOLS])
    st_h = din("st_lru_h", [16, D])
    st_lc = din("st_lru_conv", [48, D])
    st_ssd = din("st_ssd", [16, 2048, 128])
    st_sc = din("st_ssd_conv", [48, 3072])
    w_in = din("w_in", [D, 9248])
    wa_d = din("lru_wa", [16, 128, 128])
    wx_d = din("lru_wx", [16, 128, 128])
    w_out = din("w_out", [4096, D])
    w_up = din("w_up", [D, 8192])
    w_down = din("w_down", [8192, D])
    g_final = din("g_final", [D])

    y_out = dout("y_out", [NU, D])
    o_plh = dout("p_lru_h", [D])
    o_tail = dout("tail", [67, 5120])
    o_pssd = dout("p_ssd", [2048, 128])
    o_slh = dout("s_lru_h", [16, D])
    o_sssd = dout("s_ssd", [16, 2048, 128])

    u_dram = dscr("u_dram", [32, 128, NU], BF16)
    ac_dram = dscr("ac_dram", [32, NM])
    x1_dram = dscr("x1_dram", [NU, D])

    with ExitStack() as st:
        S = Sched(nc, st)
        NAR = 105600
        arena_t = st.enter_context(nc.sbuf_tensor("arena", [128, NAR], BF16))
        banks = [st.enter_context(nc.psum_tensor("ps%d" % i, [128, 512], F32)) for i in range(8)]

        ARENA_HW = [0]

        class Arena:
            def __init__(self):
                self.off = 0

            def alloc(self, shape, dt):
                n = int(np.prod(shape[1:]))
                nb = n * (2 if dt == F32 else 1)
                off = (self.off + 31) // 32 * 32
                assert off + nb <= NAR, ("arena overflow", off, nb)
                v = arena_t[:, off:off + nb]
                if dt == F32:
                    v = v.bitcast(F32)
                if shape[0] < 128:
                    v = v[0:shape[0]]
                if len(shape) == 3:
                    v = v.rearrange("p (a b) -> p a b", a=shape[1])
                elif len(shape) == 4:
                    v = v.rearrange("p (a b c) -> p a b c", a=shape[1], b=shape[2])
                self.off = off + nb
                ARENA_HW[0] = max(ARENA_HW[0], self.off)
                return v

        A = Arena()
        psi = [0]

        def nps():
            b = psi[0] % 8
            psi[0] += 1
            return banks[b], ("ps", b)

        def bf_bank(bk):
            return bk.bitcast(BF16)

        def tt(out, a, b, op, r, w, eng="dve"):
            S.add(eng, lambda e: e.tensor_tensor(out=out, in0=a, in1=b, op=op), r, w)

        def ts(out, a, s1, s2, op0, op1, r, w, eng="dve"):
            if s2 is None:
                S.add(eng, lambda e: e.tensor_scalar(out=out, in0=a, scalar1=s1, scalar2=None, op0=op0), r, w)
            else:
                S.add(eng, lambda e: e.tensor_scalar(out=out, in0=a, scalar1=s1, scalar2=s2, op0=op0, op1=op1), r, w)

        def stt(out, a, s, b, op0, op1, r, w, eng="dve"):
            S.add(eng, lambda e: e.scalar_tensor_tensor(out=out, in0=a, scalar=s, in1=b, op0=op0, op1=op1), r, w)

        def actf(out, in_, func, r, w, bias=None, scale=None, accum=None):
            kw = {}
            if bias is not None:
                kw["bias"] = bias
            if scale is not None:
                kw["scale"] = scale
            if accum is not None:
                kw["accum_out"] = accum
            S.add("act", lambda e: e.activation(out=out, in_=in_, func=func, **kw), r, w)

        def cp(out, in_, r, w, eng="dve"):
            if eng == "act":
                S.add("act", lambda e: e.copy(out=out, in_=in_), r, w)
            else:
                S.add(eng, lambda e: e.tensor_copy(out=out, in_=in_), r, w)

        def mset(ap, val, w, eng="dve"):
            S.add(eng, lambda e: e.memset(ap, val), (), w)

        def mm(out, lhsT, rhs, start, stop, r, w):
            S.add("pe", lambda e: e.matmul(out, lhsT=lhsT, rhs=rhs, start=start, stop=stop), r, w)

        def tr(out, in_, ident, r, w):
            S.add("pe", lambda e: e.transpose(out=out, in_=in_, identity=ident), r, w)

        def dma(q, out, in_, r, w):
            S.add(q, lambda e: e.dma_start(out=out, in_=in_), r, w, dma=True)

        def dma_nc(q, out, in_, r, w):
            S.add(q, lambda e: e.dma_start(out=out, in_=in_, allow_slow_non_contiguous=True), r, w, dma=True)

        MUL, ADD, SUB, MAX, MIN, POW = ALU.mult, ALU.add, ALU.subtract, ALU.max, ALU.min, ALU.pow

        cols = A.alloc([128, NCOLS], F32)
        ident_b = A.alloc([128, 128], BF16)
        ident_f = A.alloc([128, 128], F32)
        ones_b = A.alloc([128, 128], BF16)
        ones_f = A.alloc([128, 128], F32)
        negmask = A.alloc([128, 128], F32)
        negmask_s = A.alloc([64, 64], F32)
        cneg = A.alloc([128, 16], F32)
        acol = A.alloc([32, 1], F32)
        h0col = A.alloc([128, 16], F32)
        hmeta = A.alloc([128, 16], F32)
        hT = A.alloc([128, 2048], F32)
        hTm = A.alloc([128, 2048], F32)
        tmpc = A.alloc([128, 16], F32)
        epsc = A.alloc([128, 1], F32)
        S.pool(lambda e: e.memset(epsc, EPS), (), ["epsc"])

        dma("sp", cols, cols_d, (), ["cols"])
        for t_, key in ((ident_b, "ident_b"), (ident_f, "ident_f")):
            S.pool(lambda e, t_=t_: e.memset(t_, 1.0), (), [key])
            S.pool(lambda e, t_=t_: e.affine_select(out=t_, in_=t_, pattern=[[-1, 128]], compare_op=ALU.is_equal,
                                                    fill=0.0, base=0, channel_multiplier=1), [key], [key])
        S.pool(lambda e: e.memset(ones_b, 1.0), (), ["ones_b"])
        S.pool(lambda e: e.memset(ones_f, 1.0), (), ["ones_f"])
        S.pool(lambda e: e.memset(negmask, 0.0), (), ["negmask"])
        S.pool(lambda e: e.affine_select(out=negmask, in_=negmask, pattern=[[1, 128]], compare_op=ALU.is_ge,
                                         fill=NEG, base=0, channel_multiplier=-1), ["negmask"], ["negmask"])
        S.pool(lambda e: e.memset(negmask_s, 0.0), (), ["negmask_s"])
        nms3 = negmask_s.rearrange("p (s t) -> p s t", t=4)
        S.pool(lambda e: e.affine_select(out=nms3, in_=nms3, pattern=[[4, 16], [1, 4]], compare_op=ALU.is_ge,
                                         fill=NEG, base=0, channel_multiplier=-1), ["negmask_s"], ["negmask_s"])
        S.pool(lambda e: e.affine_select(out=nms3, in_=nms3, pattern=[[-4, 16], [0, 4]], compare_op=ALU.is_ge,
                                         fill=NEG, base=0, channel_multiplier=1), ["negmask_s"], ["negmask_s"])
        actf(cneg, cols[:, C_LAM:C_LAM + 16], AF.Exp, ["cols"], ["cneg"], scale=-1.0)
        actf(cneg, cneg, AF.Ln, ["cneg"], ["cneg"], bias=1.0, scale=1.0)
        ts(cneg, cneg, -8.0, None, MUL, None, ["cneg"], ["cneg"])
        actf(acol, cols[0:32, C_ALOG:C_ALOG + 1], AF.Exp, ["cols"], ["acol"])
        ts(acol, acol, -1.0, None, MUL, None, ["acol"], ["acol"])
        A_small = dict(hba=A.alloc([128, 16], F32), hbx=A.alloc([128, 16], F32), hc=A.alloc([128, 16], F32))
        ts(A_small["hba"], cols[:, C_BA:C_BA + 16], 0.5, None, MUL, None, ["cols"], ["hb"])
        ts(A_small["hbx"], cols[:, C_BX:C_BX + 16], 0.5, None, MUL, None, ["cols"], ["hb"])
        ts(A_small["hc"], cneg, 0.5, None, MUL, None, ["cneg"], ["hb"])
        csh = A.alloc([128, 120], F32)
        ts(csh, cols[:, C_CSW:C_CSW + 120], 0.5, None, MUL, None, ["cols"], ["csh"])
        base_off = A.off

        NSLAB = 3
        wslab = [A.alloc([128, 4096], BF16) for _ in range(NSLAB)]
        wsi = [0]

        def load_w(src_ap_3d, kc, ncol):
            i = wsi[0] % NSLAB
            wsi[0] += 1
            v = wslab[i][:, 0:kc * ncol].rearrange("p (k n) -> p k n", k=kc)
            key = ("wslab", i)
            dma("pool", v, src_ap_3d, (), [key])
            return v, key

        def w_in_cols(c0, ncol):
            return w_in[:, c0:c0 + ncol].rearrange("(k p) n -> p k n", p=128)

        reserved = set()

        def nps():
            while True:
                b = psi[0] % 8
                psi[0] += 1
                if b not in reserved:
                    return banks[b], ("ps", b)

        xnT = A.alloc([128, 16, NM], BF16)
        PH = A.off

        def norm_T(src, n_tok, gc0, dstT, dkey, pre=None, c0T=0):
            xt = [A.alloc([128, D], F32) for _ in range(2)] if pre is None else None
            xb = [A.alloc([128, D], BF16) for _ in range(2)]
            ss = [A.alloc([128, 2], F32) for _ in range(2)]
            for ti, r0 in enumerate(range(0, n_tok, 128)):
                n = min(128, n_tok - r0)
                s = ti % 2
                kb, ks = ("nxb", s), ("nss", s)
                if pre is None:
                    kx = ("nxt", s)
                    xin = xt[s][0:n]
                    dma("sp", xin, src[r0:r0 + n, :], (), [kx])
                else:
                    xin, kx = pre[ti][0][0:n], pre[ti][1]
                mset(ss[s][0:n], 0.0, [ks])
                actf(xb[s][0:n], xin, AF.Square, [kx, ks], [kb, ks], accum=ss[s][0:n, 0:1])
                actf(ss[s][0:n, 1:2], ss[s][0:n, 0:1], AF.Sqrt, [ks, "epsc"], [ks], bias=epsc[0:n, 0:1], scale=1.0 / D)
                S.dve(lambda e, s=s, n=n: e.reciprocal(out=ss[s][0:n, 1:2], in_=ss[s][0:n, 1:2]), [ks], [ks])
                ts(xb[s][0:n], xin, ss[s][0:n, 1:2], None, MUL, None, [kx, ks, kb], [kb])
                for half in range(2):
                    bk, pk = nps()
                    pst = bf_bank(bk).rearrange("p (a b) -> p a b", a=8)
                    for q in range(8):
                        kc = half * 8 + q
                        tr(pst[:, q, 0:n], xb[s][0:n, kc * 128:(kc + 1) * 128], ident_b[0:n, 0:n], [kb, "ident_b"], [pk])
                    gbc = cols[:, gc0 + half * 8:gc0 + half * 8 + 8].unsqueeze(2).to_broadcast([128, 8, n])
                    tt(dstT[:, half * 8:half * 8 + 8, c0T + r0:c0T + r0 + n], pst[:, :, 0:n], gbc, MUL, [pk, "cols"], [(dkey, ti)])
            return [(dkey, ti) for ti in range((n_tok + 127) // 128)]

        def tile_keys(keys, c0, n):
            return [keys[t] for t in range(c0 // 128, (c0 + n - 1) // 128 + 1)]

        def conv4(src, wcol, bcol, out, rkeys, wkey, tab=None, tkey="cols"):
            tab = cols if tab is None else tab
            ts(out, src(0), tab[:, wcol(0):wcol(0) + 1], tab[:, bcol:bcol + 1], MUL, ADD, rkeys + [tkey], [wkey])
            for k in range(1, 4):
                stt(out, src(k), tab[:, wcol(k):wcol(k) + 1], out, MUL, ADD, rkeys + [tkey, wkey], [wkey])

        DGS = {}

        def get_diag(tag, wcols):
            if DGS.get("tag") == tag:
                return DGS["ap"], DGS["key"]
            i = DGS.get("i", 0)
            DGS["i"] = i + 1
            dg = DGS["bufs"][i % 2]
            key = ("dg", i % 2)
            for k in range(4):
                ts(dg[:, k, :], ident_b, cols[:, wcols[k]:wcols[k] + 1], None, MUL, None, ["ident_b", "cols"], [key])
            DGS.update(tag=tag, ap=dg, key=key)
            return dg, key

        NLT = 3
        L = {}

        def lru_setup():
            L["lt"] = [dict(xl=A.alloc([128, 512], F32), xlb=A.alloc([128, 512], BF16), ra=A.alloc([128, 512], F32),
                            ii=A.alloc([128, 512], F32), m=A.alloc([128, 512], F32), hs=A.alloc([128, 512], F32),
                            gl=A.alloc([128, 512], F32), uo=A.alloc([128, 512], BF16)) for _ in range(NLT)]
            L["i"] = 0
            L["wab"] = A.alloc([128, 16, 128], BF16)
            L["wxb"] = A.alloc([128, 16, 128], BF16)
            dma("pool", L["wab"], wa_d.rearrange("h i j -> i h j"), (), ["wab"])
            dma("pool", L["wxb"], wx_d.rearrange("h i j -> i h j"), (), ["wxb"])
            L["uraw"] = [A.alloc([128, 4 + NM], BF16) for _ in range(2)]
            for i in range(2):
                mset(L["uraw"][i][:, 0:3], 0.0, [("uraw", i, -1)])
            DGS["bufs"] = [A.alloc([128, 4, 128], BF16) for _ in range(2)]
            DGS["tag"] = None

        hba = A_small["hba"]
        hbx = A_small["hbx"]
        hc = A_small["hc"]

        def lru_A(h, n, conv_src, conv_keys, sample=False):
            s = L["i"] % NLT
            L["i"] += 1
            T = L["lt"][s]
            wab, wxb = L["wab"], L["wxb"]
            K = lambda nm: ("lt", nm, s)

            def v(ap):
                a = ap[:, 0:n]
                return a.rearrange("p (s t) -> p s t", t=4) if sample else a
            xl, xlb, ra, ii, m = (T[k] for k in ("xl", "xlb", "ra", "ii", "m"))
            loc = {}

            def s1():
                dg, dgk = get_diag(("lru", h, id(L["lt"])), [C_CLW + k * 16 + h for k in range(4)])
                bk, pk = nps()
                for k in range(4):
                    mm(v(bk), dg[:, k, :], conv_src(k), k == 0, k == 3, conv_keys + [dgk], [pk])
                bcol = cols[:, C_CLB + h:C_CLB + h + 1]
                actf(v(xl), v(bk), AF.Identity, [pk, "cols"], [K("xl")], bias=bcol, scale=1.0)
                actf(v(xlb), v(bk), AF.Identity, [pk, "cols"], [K("xlb")], bias=bcol, scale=1.0)

            def s2():
                pass

            def s3():
                bk1, pk1 = nps()
                mm(bk1[:, 0:n], wab[:, h, :], xlb[:, 0:n], True, True, ["wab", K("xlb")], [pk1])
                bk2, pk2 = nps()
                mm(bk2[:, 0:n], wxb[:, h, :], xlb[:, 0:n], True, True, ["wxb", K("xlb")], [pk2])
                loc.update(bk1=bk1, pk1=pk1, bk2=bk2, pk2=pk2)

            def s4():
                actf(ra[:, 0:n], loc["bk1"][:, 0:n], AF.Tanh, [loc["pk1"], "hb"], [K("ra")], bias=hba[:, h:h + 1], scale=0.5)
                actf(ii[:, 0:n], loc["bk2"][:, 0:n], AF.Tanh, [loc["pk2"], "hb"], [K("ii")], bias=hbx[:, h:h + 1], scale=0.5)
                actf(ra[:, 0:n], ra[:, 0:n], AF.Exp, [K("ra"), "hb"], [K("ra")], bias=hc[:, h:h + 1], scale=hc[:, h:h + 1])

            def s5():
                tt(m[:, 0:n], ra[:, 0:n], ra[:, 0:n], MUL, [K("ra")], [K("m")])
            def s34():
                s3()
                s4()
            return dict(s=s, h=h, n=n, sample=sample, v=v, K=K, stages=[s1, s2, s34, s5])

        def run_stages(sts):
            for k in range(4):
                for st_ in sts:
                    st_["stages"][k]()

        def lru_sqrt(st, first):
            T = L["lt"][st["s"]]
            m, n, K = T["m"], st["n"], st["K"]
            actf(m[:, 0:n], m[:, 0:n], AF.Sqrt, [K("m")], [K("m")], bias=1.0, scale=-1.0)
            if first:
                mset(m[:, 0:1], 1.0, [K("m")])

        def lru_B(st, init, init_keys, gate_ps=None, gate_key=None, h0s=None):
            T = L["lt"][st["s"]]
            n, K, v, sample = st["n"], st["K"], st["v"], st["sample"]
            xl, ra, ii, m, hs, gl, uo = (T[k] for k in ("xl", "ra", "ii", "m", "hs", "gl", "uo"))
            stt(m[:, 0:n], ii[:, 0:n], 1.0, m[:, 0:n], ADD, MUL, [K("m"), K("ii")], [K("m")])
            stt(m[:, 0:n], m[:, 0:n], 0.5, xl[:, 0:n], MUL, MUL, [K("m"), K("xl")], [K("m")])
            if not sample:
                S.dve(lambda e: e.tensor_tensor_scan(out=hs[:, 0:n], data0=ra[:, 0:n], data1=m[:, 0:n], initial=init,
                                                     op0=MUL, op1=ADD), [K("ra"), K("m")] + init_keys, [K("hs")])
            else:
                a3, b3, h3 = v(ra), v(m), v(hs)
                for t_ in range(4):
                    prev = h0s if t_ == 0 else h3[:, :, t_ - 1]
                    tt(h3[:, :, t_], a3[:, :, t_], prev, MUL, [K("ra"), K("hs")] + init_keys, [K("hs")])
                    tt(h3[:, :, t_], h3[:, :, t_], b3[:, :, t_], ADD, [K("hs"), K("m")], [K("hs")])
            if gate_ps is not None:
                actf(gl[:, 0:n], gate_ps, AF.Square, [gate_key], [K("gl")])
                ts(gl[:, 0:n], gl[:, 0:n], 0.044715, 1.0, MUL, ADD, [K("gl")], [K("gl")])
                tt(gl[:, 0:n], gl[:, 0:n], gate_ps, MUL, [K("gl"), gate_key], [K("gl")])
                actf(gl[:, 0:n], gl[:, 0:n], AF.Tanh, [K("gl")], [K("gl")], scale=0.7978845608)
                stt(gl[:, 0:n], gl[:, 0:n], 1.0, gate_ps, ADD, MUL, [K("gl"), gate_key], [K("gl")])
                stt(uo[:, 0:n], gl[:, 0:n], 0.5, hs[:, 0:n], MUL, MUL, [K("gl"), K("hs")], [K("uo")])
            return hs, K("hs"), uo, K("uo")

        PT = [(0, 16), (16, 512), (528, 512)]
        PCH = [(0, 16)] + [(16 + 128 * k, 128) for k in range(8)]
        MT = [(4, 512), (516, 512)]
        MCH = [(4 + 128 * k, 128) for k in range(8)]
        ST = (1028, 64)

        A.off = PH
        pre_keys = norm_T(xp, NPRE, C_GMIX, xnT, "xnTp")
        lru_setup()
        for hq in range(8):
            xs_, xk = load_w(w_in_cols(O1 + hq * 256, 256), 16, 256)
            for hh in range(2):
                h = hq * 2 + hh
                ub = h % 2
                ur = L["uraw"][ub]
                sts = []
                for ti, (c0, n) in enumerate(PT):
                    bk, pk = nps()
                    for kc in range(16):
                        mm(bk[:, 0:n], xs_[:, kc, hh * 128:(hh + 1) * 128], xnT[:, kc, c0:c0 + n], kc == 0, kc == 15,
                           [xk] + tile_keys(pre_keys, c0, n), [pk])
                    cp(ur[:, 3 + c0:3 + c0 + n], bk[:, 0:n], [pk], [("uraw", ub, ti)], eng="act")
                    sts.append(lru_A(h, n, lambda k, ur=ur, c0=c0, n=n: ur[:, c0 + k:c0 + k + n], [("uraw", ub, ti), ("uraw", ub, ti - 1)]))
                run_stages(sts)
                for ti in range(3):
                    lru_sqrt(sts[ti], ti == 0)
                prev_hs, prev_key = None, None
                for ti, (c0, n) in enumerate(PT):
                    init = 0.0 if ti == 0 else prev_hs[:, PT[ti - 1][1] - 1:PT[ti - 1][1]]
                    ikeys = [] if ti == 0 else [prev_key]
                    hs, hk, _, _ = lru_B(sts[ti], init, ikeys)
                    if ti == 0:
                        cp(hmeta[:, h:h + 1], hs[:, n - 1:n], [hk], ["hmeta"])
                    prev_hs, prev_key = hs, hk
                cp(h0col[:, h:h + 1], prev_hs[:, 511:512], [prev_key], ["h0col"])
        tt(tmpc, h0col, hmeta, SUB, ["h0col", "hmeta"], ["tmpc"])
        stt(h0col, tmpc, cols[:, C_FLAG:C_FLAG + 1], hmeta, MUL, ADD, ["tmpc", "cols", "hmeta"], ["h0col"])

        Z = {}

        def ssd_setup(main):
            Z["dtT"] = A.alloc([32, NM], F32)
            Z["dA"] = A.alloc([32, NM], F32)
            Z["acum"] = A.alloc([32, NM], F32)
            Z["decs"] = A.alloc([32, 16], F32)
            Z["dg"] = [A.alloc([32, 32], F32) for _ in range(2)]
            Z["decbc"] = A.alloc([128, 10, 32], F32)
            Z["tok"] = A.alloc([128, 10, 96], F32)
            if main:
                Z["xs_s"] = A.alloc([128, 16, 64], BF16)
                Z["zs_s"] = A.alloc([128, 16, 64], BF16)
                Z["Bs_s"] = A.alloc([128, 4, 64], BF16)
                Z["Cs_s"] = A.alloc([128, 4, 64], BF16)
                Z["mark_s"] = A.off
            Z["xs"] = [A.alloc([128, 4, NM], BF16) for _ in range(2)]
            Z["Bs"] = [A.alloc([128, NM], BF16) for _ in range(2)]
            Z["uraw"] = [A.alloc([128, 4 + NM], BF16) for _ in range(2)]
            for i in range(2):
                mset(Z["uraw"][i][:, 0:3], 0.0, [("uraw", i, -1)])
            DGS["bufs"] = [A.alloc([128, 4, 128], BF16) for _ in range(2)]
            DGS["tag"] = None
            Z["uri"] = 0
            Z["ctmp"] = [A.alloc([128, 512], F32) for _ in range(3)]
            Z["cti"] = 0
            Z["xw"] = [A.alloc([128, 512], BF16) for _ in range(2)]
            Z["Bt"] = [A.alloc([128, 128], BF16) for _ in range(2)]
            Z["stmp"] = [A.alloc([128, 512], F32) for _ in range(1)]
            if main:
                Z["Cs"] = [A.alloc([128, NM], BF16) for _ in range(2)]
                Z["zs"] = [A.alloc([128, 4, NU], BF16) for _ in range(2)]
                Z["hTb"] = A.alloc([128, 2048], BF16)
                Z["Rb"] = [A.alloc([128, 8, 128], F32) for _ in range(2)]
                Z["Eo"] = [A.alloc([128, 8, 128], BF16) for _ in range(2)]
                Z["Lm"] = [A.alloc([128, 8, 128], BF16) for _ in range(2)]
                Z["Csc"] = [A.alloc([128, 8, 128], BF16) for _ in range(2)]
                Z["cbT"] = [A.alloc([128, 128], BF16) for _ in range(2)]
                Z["xdt"] = [A.alloc([128, 512], BF16) for _ in range(2)]
                Z["yt"] = [A.alloc([128, 4, 128], F32) for _ in range(2)]
                Z["uo"] = [A.alloc([128, 4, 128], BF16) for _ in range(2)]
                Z["scT"] = A.alloc([128, 24, 48], F32)
                Z["uext"] = [A.alloc([128, 16, 8], BF16) for _ in range(2)]
                Z["tailst"] = [A.alloc([67, 256], F32) for _ in range(2)]
                Z["tsi"] = 0

        def ssd_dt(xkeys, tiles, chunks, main):
            dtT, dA, acum, decs, dg, decbc, tok = (Z[k] for k in ("dtT", "dA", "acum", "decs", "dg", "decbc", "tok"))
            ds_, dk = load_w(w_in_cols(O4, 32), 16, 32)
            for (c0, n) in tiles:
                bk, pk = nps()
                for kc in range(16):
                    mm(bk[0:32, 0:n], ds_[:, kc, :], xnT[:, kc, c0:c0 + n], kc == 0, kc == 15, [dk] + tile_keys(xkeys, c0, n), [pk])
                actf(dtT[:, c0:c0 + n], bk[0:32, 0:n], AF.Exp, [pk, "cols"], ["dtT"], bias=cols[0:32, C_DTB:C_DTB + 1], scale=1.0)
                actf(dtT[:, c0:c0 + n], dtT[:, c0:c0 + n], AF.Ln, ["dtT"], ["dtT"], bias=1.0, scale=1.0)
                ts(dA[:, c0:c0 + n], dtT[:, c0:c0 + n], acol[:, 0:1], None, MUL, None, ["dtT", "acol"], ["dA"])
            for ci, (c0, q) in enumerate(chunks):
                S.dve(lambda e, c0=c0, q=q: e.tensor_tensor_scan(out=acum[:, c0:c0 + q], data0=ones_f[0:32, 0:q], data1=dA[:, c0:c0 + q],
                                                                 initial=0.0, op0=MUL, op1=ADD), ["ones_f", "dA"], ["acum"])
                last = acum[:, c0 + q - 1:c0 + q]
                actf(dA[:, c0:c0 + q], acum[:, c0:c0 + q], AF.Exp, ["acum", "dA"], ["dA"], bias=last, scale=-1.0)
                tt(dA[:, c0:c0 + q], dA[:, c0:c0 + q], dtT[:, c0:c0 + q], MUL, ["dA", "dtT"], ["dA"])
                actf(decs[:, ci:ci + 1], last, AF.Exp, ["acum"], ["decs"])
                ts(dg[ci % 2], ident_f[0:32, 0:32], decs[:, ci:ci + 1], None, MUL, None, ["ident_f", "decs"], [("dg", ci % 2)])
                bk, pk = nps()
                mm(bk[:, 0:32], ones_f[0:32, :], dg[ci % 2], True, True, ["ones_f", ("dg", ci % 2)], [pk])
                cp(decbc[:, ci, :], bk[:, 0:32], [pk], ["decbc"], eng="act")
                bk, pk = nps()
                tr(bk[0:q, 0:32], dtT[:, c0:c0 + q], ident_f[0:32, 0:32], ["dtT", "ident_f"], [pk])
                tr(bk[0:q, 32:64], dA[:, c0:c0 + q], ident_f[0:32, 0:32], ["dA", "ident_f"], [pk])
                tr(bk[0:q, 64:96], acum[:, c0:c0 + q], ident_f[0:32, 0:32], ["acum", "ident_f"], [pk])
                cp(tok[0:q, ci, :], bk[0:q, 0:96], [pk], ["tok"])
            if main:
                c0 = ST[0]
                a3 = acum[:, c0:c0 + 64].rearrange("p (s t) -> p s t", t=4)
                d3 = dA[:, c0:c0 + 64].rearrange("p (s t) -> p s t", t=4)
                t3 = dtT[:, c0:c0 + 64].rearrange("p (s t) -> p s t", t=4)
                cp(a3[:, :, 0], d3[:, :, 0], ["dA"], ["acum"])
                for t in range(1, 4):
                    tt(a3[:, :, t], a3[:, :, t - 1], d3[:, :, t], ADD, ["acum", "dA"], ["acum"])
                tt(d3, a3[:, :, 3:4].to_broadcast([32, 16, 4]), a3, SUB, ["acum", "dA"], ["dA"])
                actf(dA[:, c0:c0 + 64], dA[:, c0:c0 + 64], AF.Exp, ["dA"], ["dA"])
                tt(d3, d3, t3, MUL, ["dA", "dtT"], ["dA"])
                ci = 8
                bk, pk = nps()
                tr(bk[0:64, 0:32], dtT[:, c0:c0 + 64], ident_f[0:32, 0:32], ["dtT", "ident_f"], [pk])
                tr(bk[0:64, 32:64], dA[:, c0:c0 + 64], ident_f[0:32, 0:32], ["dA", "ident_f"], [pk])
                tr(bk[0:64, 64:96], acum[:, c0:c0 + 64], ident_f[0:32, 0:32], ["acum", "ident_f"], [pk])
                cp(tok[0:64, ci, :], bk[0:64, 0:96], [pk], ["tok"])
                dma("sp", ac_dram[:, :], acum[:, :], ["acum"], ["ac_dram"])

        def ssd_chan(sl, sk, off, chidx, dst, dkey, tiles, xkeys, main):
            ub = Z["uri"] % 2
            Z["uri"] += 1
            ur = Z["uraw"][ub]
            if main:
                bk, pk = nps()
                for kc in range(16):
                    mm(bk[:, 0:4], sl[:, kc, off:off + 128], xnT[:, kc, 0:4], kc == 0, kc == 15, [sk, xkeys[0]], [pk])
                cp(ur[:, 3:7], bk[:, 0:4], [pk], [("uraw", ub, -1)], eng="act")
            work = []
            for ti, (c0, n) in enumerate(tiles):
                bk, pk = nps()
                for kc in range(16):
                    mm(bk[:, 0:n], sl[:, kc, off:off + 128], xnT[:, kc, c0:c0 + n], kc == 0, kc == 15, [sk] + tile_keys(xkeys, c0, n), [pk])
                cp(ur[:, 3 + c0:3 + c0 + n], bk[:, 0:n], [pk], [("uraw", ub, ti)], eng="act")
                cs = Z["cti"] % 3
                Z["cti"] += 1
                work.append((False, ti, c0, n, cs))
            if main:
                c0, n = ST
                bk, pk = nps()
                for kc in range(16):
                    mm(bk[:, 0:n], sl[:, kc, off:off + 128], xnT[:, kc, c0:c0 + n], kc == 0, kc == 15, [sk] + tile_keys(xkeys, c0, n), [pk])
                ue = Z["uext"][ub]
                cp(ue[:, :, 0:3], Z["scT"][:, chidx, :].rearrange("p (s k) -> p s k", k=3), ["scT"], [("uext", ub)])
                cp(ue[:, :, 3:7], bk[:, 0:64].rearrange("p (s t) -> p s t", t=4), [pk], [("uext", ub)], eng="act")
                cs = Z["cti"] % 3
                Z["cti"] += 1
                work.append((True, -9, c0, 64, cs))
            dg, dgk = get_diag(("ssd", chidx, id(Z["uraw"])), [C_CSW + k * 24 + chidx for k in range(4)])
            bcol = csh[:, 96 + chidx:96 + chidx + 1]
            for (smp, ti, c0, n, cs) in work:
                ct = Z["ctmp"][cs]
                bk, pk = nps()
                if not smp:
                    for k in range(4):
                        mm(bk[:, 0:n], dg[:, k, :], ur[:, c0 + k:c0 + k + n], k == 0, k == 3, [("uraw", ub, ti), ("uraw", ub, ti - 1), dgk], [pk])
                else:
                    for k in range(4):
                        mm(bk[:, 0:64].rearrange("p (s t) -> p s t", t=4), dg[:, k, :], ue[:, :, k:k + 4], k == 0, k == 3, [("uext", ub), dgk], [pk])
                actf(dst[:, c0:c0 + n], bk[:, 0:n], AF.Tanh, [pk, "csh"], [dkey], bias=bcol, scale=0.5)
                actf(ct[:, 0:n], bk[:, 0:n], AF.Identity, [pk, "csh"], [("ctmp", cs)], bias=bcol, scale=0.5)
            for (smp, ti, c0, n, cs) in work:
                ct = Z["ctmp"][cs]
                stt(dst[:, c0:c0 + n], dst[:, c0:c0 + n], 1.0, ct[:, 0:n], ADD, MUL, [dkey, ("ctmp", cs)], [dkey])

        def tail_proj(sl, sk, ncol, ocol, xkeys):
            bk, pk = nps()
            for kc in range(16):
                mm(bk[0:67, 0:ncol], xnT[:, kc, 1025:1092], sl[:, kc, 0:ncol], kc == 0, kc == 15, [sk, xkeys[8]], [pk])
            tb = Z["tsi"] % 2
            Z["tsi"] += 1
            cp(Z["tailst"][tb][:, 0:ncol], bk[0:67, 0:ncol], [pk], [("tailst", tb)], eng="act")
            dma("sp", o_tail[:, ocol:ocol + ncol], Z["tailst"][tb][:, 0:ncol], [("tailst", tb)], ["out_tail_%d" % ocol])

        def tok_major(g, ci, c0, q, need_xdt):
            s = Z.get("cti2", 0) % 2
            Z["cti2"] = Z.get("cti2", 0) + 1
            gb = g % 2
            xs, Bs, tok = Z["xs"][gb], Z["Bs"][gb], Z["tok"]
            bk, pk = nps()
            for cc in range(4):
                mm(bk[0:q, cc * 128:(cc + 1) * 128], xs[:, cc, c0:c0 + q], ident_b, True, True, [("xs", gb, cc), "ident_b"], [pk])
            bk2, pk2 = nps()
            mm(bk2[0:q, 0:128], Bs[:, c0:c0 + q], ident_b, True, True, [("Bs", gb), "ident_b"], [pk2])
            xw, Bt = Z["xw"][s], Z["Bt"][s]
            x3 = bk[0:q, 0:512].rearrange("p (h d) -> p h d", d=64)
            wbc = tok[0:q, ci, 32 + 8 * g:32 + 8 * g + 8].unsqueeze(2).to_broadcast([q, 8, 64])
            tt(xw[0:q, :].rearrange("p (h d) -> p h d", d=64), x3, wbc, MUL, [pk, "tok"], [("xw", s)])
            xdt = None
            if need_xdt:
                xdt = Z["xdt"][s]
                dbc = tok[0:q, ci, 8 * g:8 * g + 8].unsqueeze(2).to_broadcast([q, 8, 64])
                tt(xdt[0:q, :].rearrange("p (h d) -> p h d", d=64), x3, dbc, MUL, [pk, "tok"], [("xdt", s)])
            cp(Bt[0:q, :], bk2[0:q, 0:128], [pk2], [("Bt", s)], eng="act")
            return xw, ("xw", s), Bt, ("Bt", s), xdt, ("xdt", s)

        def state_update(g, ci, q, xw, kxw, Bt, kBt, first):
            bk, pk = nps()
            mm(bk[:, 0:512], Bt[0:q, :], xw[0:q, :], True, True, [kBt, kxw], [pk])
            hg = hT[:, g * 512:(g + 1) * 512]
            if first:
                cp(hg, bk[:, 0:512], [pk], [("hT", g)])
            else:
                s = 0
                stp = Z["stmp"][s]
                dbc = Z["decbc"][:, ci, 8 * g:8 * g + 8].unsqueeze(2).to_broadcast([128, 8, 64])
                tt(stp.rearrange("p (h d) -> p h d", d=64), hg.rearrange("p (h d) -> p h d", d=64), dbc, MUL,
                   [("hT", g), "decbc"], [("stmp", s)])
                tt(hg, stp, bk[:, 0:512], ADD, [("stmp", s), pk], [("hT", g)])

        if stage >= 2:
            S.barrier()
            A.off = PH
            ssd_setup(False)
            ssd_dt(pre_keys, PT, PCH, False)
            for g in range(4 if stage >= 2.2 else 0):
                for sq in range(2):
                    sl, sk = load_w(w_in_cols(O3 + g * 512 + sq * 256, 256), 16, 256)
                    for j in range(2):
                        cc = sq * 2 + j
                        ssd_chan(sl, sk, j * 128, g * 4 + cc, Z["xs"][g % 2][:, cc, :], ("xs", g % 2, cc), PT, pre_keys, False)
                sl, sk = load_w(w_in_cols(O3 + 2048 + g * 128, 128), 16, 128)
                ssd_chan(sl, sk, 0, 16 + g, Z["Bs"][g % 2], ("Bs", g % 2), PT, pre_keys, False)
                for ci, (c0, q) in enumerate(PCH if stage >= 2.3 else []):
                    xw, kxw, Bt, kBt, _, _ = tok_major(g, ci, c0, q, False)
                    if stage < 2.4:
                        continue
                    state_update(g, ci, q, xw, kxw, Bt, kBt, ci == 0)
                    if ci == 0:
                        cp(hTm[:, g * 512:(g + 1) * 512], hT[:, g * 512:(g + 1) * 512], [("hT", g)], [("hTm", g)])
                if stage < 2.5:
                    continue
                hg, hm = hT[:, g * 512:(g + 1) * 512], hTm[:, g * 512:(g + 1) * 512]
                tt(hg, hg, hm, SUB, [("hT", g), ("hTm", g)], [("hT", g)])
                stt(hg, hg, cols[:, C_FLAG:C_FLAG + 1], hm, MUL, ADD, [("hT", g), ("hTm", g), "cols"], [("hT", g)])

        S.barrier()
        A.off = PH
        main_keys = norm_T(xm, NM, C_GMIX, xnT, "xnTm")
        lru_setup()
        hfin = A.alloc([128, 16], F32)
        hsfin = A.alloc([128, 16, 16], F32)
        lcT = A.alloc([128, 16, 48], F32)
        h0T = A.alloc([128, 16, 16], F32)
        tmp_lc = A.alloc([48, D], F32)
        tmp_h = A.alloc([16, D], F32)
        dma("sp", tmp_lc, st_lc, (), ["tmp_lc"])
        dma("sp", tmp_h, st_h, (), ["tmp_h"])
        for c in range(16):
            bk, pk = nps()
            tr(bk[:, 0:48], tmp_lc[:, c * 128:(c + 1) * 128], ident_f[0:48, 0:48], ["tmp_lc", "ident_f"], [pk])
            tr(bk[:, 64:80], tmp_h[:, c * 128:(c + 1) * 128], ident_f[0:16, 0:16], ["tmp_h", "ident_f"], [pk])
            cp(lcT[:, c, :], bk[:, 0:48], [pk], ["lcT"])
            cp(h0T[:, c, :], bk[:, 64:80], [pk], ["h0T"])
        uext = [A.alloc([128, 16, 8], BF16) for _ in range(2)]
        Z["tailst"] = [A.alloc([67, 256], F32) for _ in range(2)]
        Z["tsi"] = 0
        for hq in range(8 if (stage < 2 or stage >= 3) else 0):
            gs_, gk = load_w(w_in_cols(hq * 256, 256), 16, 256)
            xs_, xk = load_w(w_in_cols(O1 + hq * 256, 256), 16, 256)
            tail_proj(xs_, xk, 256, hq * 256, main_keys)
            for hh in range(2):
                h = hq * 2 + hh
                ub = h % 2
                ur = L["uraw"][ub]
                bk, pk = nps()
                for kc in range(16):
                    mm(bk[:, 0:4], xs_[:, kc, hh * 128:(hh + 1) * 128], xnT[:, kc, 0:4], kc == 0, kc == 15, [xk, main_keys[0]], [pk])
                cp(ur[:, 3:7], bk[:, 0:4], [pk], [("uraw", ub, -1)], eng="act")
                sts, gates = [], []
                gbanks = []
                for ti, (c0, n) in enumerate(MT + [ST]):
                    sample = ti == 2
                    bk, pk = nps()
                    for kc in range(16):
                        mm(bk[:, 0:n], xs_[:, kc, hh * 128:(hh + 1) * 128], xnT[:, kc, c0:c0 + n], kc == 0, kc == 15,
                           [xk] + tile_keys(main_keys, c0, n), [pk])
                    bg, pg = nps()
                    reserved.add(int(pg[1]))
                    gbanks.append(int(pg[1]))
                    for kc in range(16):
                        mm(bg[:, 0:n], gs_[:, kc, hh * 128:(hh + 1) * 128], xnT[:, kc, c0:c0 + n], kc == 0, kc == 15,
                           [gk] + tile_keys(main_keys, c0, n), [pg])
                    gates.append((bg, pg))
                    if not sample:
                        cp(ur[:, 3 + c0:3 + c0 + n], bk[:, 0:n], [pk], [("uraw", ub, ti)], eng="act")
                        sts.append(lru_A(h, n, lambda k, ur=ur, c0=c0, n=n: ur[:, c0 + k:c0 + k + n], [("uraw", ub, ti), ("uraw", ub, ti - 1)]))
                    else:
                        ue = uext[ub]
                        cp(ue[:, :, 0:3], lcT[:, h, :].rearrange("p (s k) -> p s k", k=3), ["lcT"], [("uext", ub)])
                        cp(ue[:, :, 3:7], bk[:, 0:64].rearrange("p (s t) -> p s t", t=4), [pk], [("uext", ub)], eng="act")
                        sts.append(lru_A(h, 64, lambda k, ue=ue: ue[:, :, k:k + 4], [("uext", ub)], sample=True))
                run_stages(sts)
                for ti in range(3):
                    lru_sqrt(sts[ti], False)
                prev_hs, prev_key = None, None
                for ti, (c0, n) in enumerate(MT + [ST]):
                    bg, pg = gates[ti]
                    if ti < 2:
                        init = h0col[:, h:h + 1] if ti == 0 else prev_hs[:, 511:512]
                        ikeys = ["h0col"] if ti == 0 else [prev_key]
                        hs, hk, uo, uk = lru_B(sts[ti], init, ikeys, gate_ps=bg[:, 0:n], gate_key=pg)
                        prev_hs, prev_key = hs, hk
                        if ti == 1:
                            cp(hfin[:, h:h + 1], hs[:, 511:512], [hk], ["hfin"])
                        dma("sp", u_dram[h, :, c0 - 4:c0 - 4 + n], uo[:, 0:n], [uk], [("u_dram", h)])
                    else:
                        hs, hk, uo, uk = lru_B(sts[ti], None, ["h0T"], gate_ps=bg[:, 0:64], gate_key=pg, h0s=h0T[:, h, :])
                        cp(hsfin[:, h, :], hs[:, 0:64].rearrange("p (s t) -> p s t", t=4)[:, :, 3], [hk], ["hsfin"])
                        dma("sp", u_dram[h, :, 1024:1088], uo[:, 0:64], [uk], [("u_dram", h)])
                for b_ in gbanks:
                    reserved.discard(b_)
        dma_nc("sp", o_plh.rearrange("(h p) -> p h", p=128), hfin, ["hfin"], ["out_plh"])
        slh = tmp_h
        for c in range(16):
            bk, pk = nps()
            tr(bk[0:16, 0:128], hsfin[:, c, :], ident_f, ["hsfin", "ident_f"], [pk])
            cp(slh[:, c * 128:(c + 1) * 128], bk[0:16, 0:128], [pk], ["tmp_h"])
        dma("sp", o_slh, slh, ["tmp_h"], ["out_slh"])

        if stage >= 3:
            S.barrier()
            A.off = PH
            ssd_setup(True)
            tmp_sc = hTm[0:48, 0:1536]
            for hf in range(2):
                dma("sp", tmp_sc, st_sc[:, hf * 1536:(hf + 1) * 1536], (), ["scr8k"])
                for c in range(12):
                    bk, pk = nps()
                    tr(bk[:, 0:48], tmp_sc[:, c * 128:(c + 1) * 128], ident_f[0:48, 0:48], ["scr8k", "ident_f"], [pk])
                    cp(Z["scT"][:, hf * 12 + c, :], bk[:, 0:48], [pk], ["scT"])
            ssd_dt(main_keys, MT + [ST], MCH, True)
            def proj_units(g):
                gb = g % 2
                xs, Bs, Cs, zs = Z["xs"][gb], Z["Bs"][gb], Z["Cs"][gb], Z["zs"][gb]
                units = []

                def uz(sq):
                    sl, sk = load_w(w_in_cols(O2 + g * 512 + sq * 256, 256), 16, 256)
                    for j in range(2):
                        cc = sq * 2 + j
                        for (c0, n) in MT + [ST]:
                            bk, pk = nps()
                            for kc in range(16):
                                mm(bk[:, 0:n], sl[:, kc, j * 128:(j + 1) * 128], xnT[:, kc, c0:c0 + n], kc == 0, kc == 15,
                                   [sk] + tile_keys(main_keys, c0, n), [pk])
                            zv = zs[:, cc, c0 - 4:c0 - 4 + n]
                            actf(zv, bk[:, 0:n], AF.Tanh, [pk], [("zs", gb, cc)], scale=0.5)
                            stt(zv, zv, 1.0, bk[:, 0:n], ADD, MUL, [pk, ("zs", gb, cc)], [("zs", gb, cc)])

                def ux(sq):
                    sl, sk = load_w(w_in_cols(O3 + g * 512 + sq * 256, 256), 16, 256)
                    tail_proj(sl, sk, 256, 2048 + g * 512 + sq * 256, main_keys)
                    for j in range(2):
                        cc = sq * 2 + j
                        ssd_chan(sl, sk, j * 128, g * 4 + cc, xs[:, cc, :], ("xs", gb, cc), MT, main_keys, True)

                def uB():
                    sl, sk = load_w(w_in_cols(O3 + 2048 + g * 128, 128), 16, 128)
                    tail_proj(sl, sk, 128, 2048 + 2048 + g * 128, main_keys)
                    ssd_chan(sl, sk, 0, 16 + g, Bs, ("Bs", gb), MT, main_keys, True)

                def uC():
                    sl, sk = load_w(w_in_cols(O3 + 2560 + g * 128, 128), 16, 128)
                    tail_proj(sl, sk, 128, 2048 + 2560 + g * 128, main_keys)
                    ssd_chan(sl, sk, 0, 20 + g, Cs, ("Cs", gb), MT, main_keys, True)

                def ustash():
                    cp(Z["xs_s"][:, 4 * g:4 * g + 4, :], xs[:, :, ST[0]:ST[0] + 64], [("xs", gb, c) for c in range(4)], ["xs_s"])
                    cp(Z["zs_s"][:, 4 * g:4 * g + 4, :], zs[:, :, 1024:1088], [("zs", gb, c) for c in range(4)], ["zs_s"])
                    cp(Z["Bs_s"][:, g, :], Bs[:, ST[0]:ST[0] + 64], [("Bs", gb)], ["Bs_s"])
                    cp(Z["Cs_s"][:, g, :], Cs[:, ST[0]:ST[0] + 64], [("Cs", gb)], ["Cs_s"])
                return [lambda: uB(), lambda: uC(), lambda: ux(0), lambda: ux(1), lambda: uz(0), lambda: uz(1), ustash]

            for u_ in proj_units(0):
                u_()
            for g in range(4):
                gb = g % 2
                xs, Bs, Cs, zs, hTb, tok = Z["xs"][gb], Z["Bs"][gb], Z["Cs"][gb], Z["zs"][gb], Z["hTb"], Z["tok"]
                nxt = proj_units(g + 1) if g < 3 else []
                hg = hT[:, g * 512:(g + 1) * 512]
                hbg = hTb[:, g * 512:(g + 1) * 512]
                cp(hbg, hg, [("hT", g)], [("hTb", g)])
                def pre(ci):
                    c0, q = MCH[ci]
                    s = ci % 2
                    Rb, Eo, Lm, Csc, cbT = (Z[k][s] for k in ("Rb", "Eo", "Lm", "Csc", "cbT"))
                    dma("sp", Rb, ac_dram[8 * g:8 * g + 8, c0:c0 + 128].partition_broadcast(128), ["ac_dram"], [("Rb", s)])
                    actf(Eo, Rb, AF.Exp, [("Rb", s)], [("Eo", s)])
                    tt(Rb, Rb, negmask.unsqueeze(1).to_broadcast([128, 8, 128]), ADD, [("Rb", s), "negmask"], [("Rb", s)])
                    tt(Rb, Rb, tok[:, ci, 64 + 8 * g:64 + 8 * g + 8].unsqueeze(2).to_broadcast([128, 8, 128]), SUB,
                       [("Rb", s), "tok"], [("Rb", s)])
                    actf(Lm, Rb, AF.Exp, [("Rb", s)], [("Lm", s)])
                    bk, pk = nps()
                    mm(bk[:, 0:128], Bs[:, c0:c0 + 128], Cs[:, c0:c0 + 128], True, True, [("Bs", gb), ("Cs", gb)], [pk])
                    cp(cbT, bk[:, 0:128], [pk], [("cbT", s)], eng="act")
                    tt(Lm, Lm, cbT.unsqueeze(1).to_broadcast([128, 8, 128]), MUL, [("Lm", s), ("cbT", s)], [("Lm", s)])
                    tt(Csc, Eo, Cs[:, c0:c0 + 128].unsqueeze(1).to_broadcast([128, 8, 128]), MUL, [("Eo", s), ("Cs", gb)], [("Csc", s)])
                    return tok_major(g, ci, c0, 128, True)

                def post(ci, tm):
                    c0, q = MCH[ci]
                    s = ci % 2
                    Lm, Csc, yt, uo = (Z[k][s] for k in ("Lm", "Csc", "yt", "uo"))
                    xw, kxw, Bt, kBt, xdt, kxdt = tm
                    by, py = nps()
                    for hp in range(4):
                        for e2 in range(2):
                            hh = hp * 2 + e2
                            o = by[64 * e2:64 * e2 + 64, hp * 128:(hp + 1) * 128]
                            mm(o, xdt[:, hh * 64:(hh + 1) * 64], Lm[:, hh, :], True, False, [kxdt, ("Lm", s)], [py])
                            mm(o, hbg[:, hh * 64:(hh + 1) * 64], Csc[:, hh, :], False, True, [("hTb", g), ("Csc", s)], [py])
                    dsk = cols[:, C_DSK + 4 * g:C_DSK + 4 * g + 4].unsqueeze(2).to_broadcast([128, 4, 128])
                    tt(yt, xs[:, :, c0:c0 + 128], dsk, MUL, [("xs", gb, c) for c in range(4)] + ["cols"], [("yt", s)])
                    tt(yt, yt, by[:, 0:512].rearrange("p (c t) -> p c t", c=4), ADD, [("yt", s), py], [("yt", s)])
                    stt(uo, yt, 0.5, zs[:, :, c0 - 4:c0 - 4 + 128], MUL, MUL, [("yt", s)] + [("zs", gb, c) for c in range(4)], [("uo", s)])
                    dma("sp", u_dram[16 + 4 * g:16 + 4 * g + 4, :, c0 - 4:c0 - 4 + 128].rearrange("c p t -> p c t"), uo,
                        [("uo", s)], [("u_dram", 16 + g)])
                    state_update(g, ci, 128, xw, kxw, Bt, kBt, False)
                    cp(hbg, hg, [("hT", g)], [("hTb", g)], eng="act")

                tm_next = pre(0)
                for ci in range(8):
                    tm_cur = tm_next
                    if ci + 1 < 8:
                        tm_next = pre(ci + 1)
                    post(ci, tm_cur)
                    if ci < len(nxt):
                        nxt[ci]()
                for u_ in nxt[8:]:
                    u_()
            pst_o = hTm.rearrange("p (c n) -> p c n", c=16)
            for c in range(16):
                bk, pk = nps()
                tr(bk[:, 0:128], hT[:, c * 128:(c + 1) * 128], ident_f, [("hT", c // 4), "ident_f"], [pk])
                cp(pst_o[:, c, :], bk[:, 0:128], [pk], ["scr8k"])
            dma("sp", o_pssd.rearrange("(c p) n -> p c n", p=128), pst_o, ["scr8k"], ["out_pssd"])

        if stage >= 5:
            S.barrier()
            A.off = Z["mark_s"]
            tok, acum = Z["tok"], Z["acum"]
            xs_s, zs_s, Bs_s, Cs_s = Z["xs_s"], Z["zs_s"], Z["Bs_s"], Z["Cs_s"]
            c0 = ST[0]
            SEL = A.alloc([32, 16, 128], F32)
            S.pool(lambda e: e.memset(SEL, 1.0), (), ["SEL"])
            SEL4 = SEL.rearrange("p c (a b) -> p c a b", a=2)
            S.pool(lambda e: e.affine_select(out=SEL4, in_=SEL4, pattern=[[-2, 16], [-1, 2], [0, 64]], compare_op=ALU.is_equal,
                                             fill=0.0, base=0, channel_multiplier=1), ["SEL"], ["SEL"])
            seqmask = A.alloc([64, 16], F32)
            S.pool(lambda e: e.memset(seqmask, 1.0), (), ["seqmask"])
            S.pool(lambda e: e.affine_select(out=seqmask, in_=seqmask, pattern=[[-4, 16]], compare_op=ALU.is_ge,
                                             fill=0.0, base=0, channel_multiplier=1), ["seqmask"], ["seqmask"])
            S.pool(lambda e: e.affine_select(out=seqmask, in_=seqmask, pattern=[[4, 16]], compare_op=ALU.is_ge,
                                             fill=0.0, base=3, channel_multiplier=-1), ["seqmask"], ["seqmask"])
            eac = A.alloc([32, 64], F32)
            actf(eac, acum[:, c0:c0 + 64], AF.Exp, ["acum"], ["eac"])
            Eb = A.alloc([128, 16, 64], F32)
            for ch in range(16):
                bk, pk = nps()
                mm(bk[:, 0:64], SEL[:, ch, :], eac, True, True, ["SEL", "eac"], [pk])
                cp(Eb[:, ch, :], bk[:, 0:64], [pk], ["Eb"], eng="act")
            Rb_s = [A.alloc([64, 8, 64], F32) for _ in range(2)]
            L_s = [A.alloc([64, 8, 64], BF16) for _ in range(2)]
            cbT_s = [A.alloc([64, 64], BF16) for _ in range(2)]
            xdt_s = A.alloc([64, 2048], BF16)
            xw_s = A.alloc([64, 2048], BF16)
            Bt_s = A.alloc([64, 4, 128], BF16)
            ydiag = A.alloc([128, 16, 64], F32)
            for g in range(4):
                s = g % 2
                dma("sp", Rb_s[s], ac_dram[8 * g:8 * g + 8, c0:c0 + 64].partition_broadcast(64), ["ac_dram"], [("Rb_s", s)])
                tt(Rb_s[s], Rb_s[s], negmask_s.unsqueeze(1).to_broadcast([64, 8, 64]), ADD, [("Rb_s", s), "negmask_s"], [("Rb_s", s)])
                tt(Rb_s[s], Rb_s[s], tok[0:64, 8, 64 + 8 * g:64 + 8 * g + 8].unsqueeze(2).to_broadcast([64, 8, 64]), SUB,
                   [("Rb_s", s), "tok"], [("Rb_s", s)])
                actf(L_s[s], Rb_s[s], AF.Exp, [("Rb_s", s)], [("L_s", s)])
                bk, pk = nps()
                mm(bk[0:64, 0:64], Bs_s[:, g, :], Cs_s[:, g, :], True, True, ["Bs_s", "Cs_s"], [pk])
                cp(cbT_s[s], bk[0:64, 0:64], [pk], [("cbT_s", s)], eng="act")
                tt(L_s[s], L_s[s], cbT_s[s].unsqueeze(1).to_broadcast([64, 8, 64]), MUL, [("L_s", s), ("cbT_s", s)], [("L_s", s)])
                bk, pk = nps()
                for cc in range(4):
                    mm(bk[0:64, cc * 128:(cc + 1) * 128], xs_s[:, 4 * g + cc, :], ident_b, True, True, ["xs_s", "ident_b"], [pk])
                x3 = bk[0:64, 0:512].rearrange("p (h d) -> p h d", d=64)
                tt(xdt_s[:, g * 512:(g + 1) * 512].rearrange("p (h d) -> p h d", d=64), x3,
                   tok[0:64, 8, 8 * g:8 * g + 8].unsqueeze(2).to_broadcast([64, 8, 64]), MUL, [pk, "tok"], [("xdt_s", g)])
                tt(xw_s[:, g * 512:(g + 1) * 512].rearrange("p (h d) -> p h d", d=64), x3,
                   tok[0:64, 8, 32 + 8 * g:32 + 8 * g + 8].unsqueeze(2).to_broadcast([64, 8, 64]), MUL, [pk, "tok"], [("xw_s", g)])
                bk2, pk2 = nps()
                mm(bk2[0:64, 0:128], Bs_s[:, g, :], ident_b, True, True, ["Bs_s", "ident_b"], [pk2])
                cp(Bt_s[:, g, :], bk2[0:64, 0:128], [pk2], ["Bt_s"], eng="act")
                by, py = nps()
                for hp in range(4):
                    for e2 in range(2):
                        hh = hp * 2 + e2
                        mm(by[64 * e2:64 * e2 + 64, hp * 64:(hp + 1) * 64], xdt_s[:, g * 512 + hh * 64:g * 512 + (hh + 1) * 64], L_s[s][:, hh, :],
                           True, True, [("xdt_s", g), ("L_s", s)], [py])
                cp(ydiag[:, 4 * g:4 * g + 4, :], by[:, 0:256].rearrange("p (c t) -> p c t", c=4), [py], ["ydiag"])
            yb = []
            while len(yb) < 2:
                b = psi[0] % 8
                psi[0] += 1
                if b not in reserved:
                    reserved.add(b)
                    yb.append(b)
            NBS = 4
            stb = [A.alloc([128, 16, 128], F32) for _ in range(NBS)]
            stT = [A.alloc([128, 2048], BF16) for _ in range(NBS)]
            Bm = [A.alloc([64, 512], BF16) for _ in range(NBS)]
            Bt_flat = Bt_s.rearrange("p g n -> p (g n)")

            def ld_state(q_):
                dma("sp", stb[q_ % NBS], st_ssd[q_].rearrange("(c p) n -> p c n", p=128), (), [("stb", q_ % NBS)])
            for q_ in range(NBS - 1):
                ld_state(q_)
            for sq_ in range(16):
                s = sq_ % NBS
                ks = ("stb", s)
                if sq_ + NBS - 1 < 16:
                    ld_state(sq_ + NBS - 1)
                for q4 in range(4):
                    bk, pk = nps()
                    for j in range(4):
                        tr(bk[:, j * 128:(j + 1) * 128], stb[s][:, 4 * q4 + j, :], ident_f, [ks, "ident_f"], [pk])
                    cp(stT[s][:, q4 * 512:(q4 + 1) * 512], bk[:, 0:512], [pk], [("stT", s, q4)], eng="act")
                ybk = banks[yb[sq_ // 8]]
                ykey = ("ps", yb[sq_ // 8])
                for ch in range(16):
                    off = (sq_ % 8) * 64 + ch * 4
                    mm(ybk[:, off:off + 4], stT[s][:, ch * 128:(ch + 1) * 128], Cs_s[:, ch // 4, 4 * sq_:4 * sq_ + 4], True, True,
                       [("stT", s, ch // 4), "Cs_s"], [ykey])
                ts(Bm[s], Bt_flat, seqmask[:, sq_:sq_ + 1], None, MUL, None, ["Bt_s", "seqmask"], [("Bm", s)])
                for q4 in range(4):
                    bk, pk = nps()
                    for j in range(4):
                        ch = 4 * q4 + j
                        mm(bk[:, j * 128:(j + 1) * 128], xw_s[:, ch * 128:(ch + 1) * 128], Bm[s][:, (ch // 4) * 128:(ch // 4 + 1) * 128],
                           True, True, [("xw_s", ch // 4), ("Bm", s)], [pk])
                    sv = stb[s][:, 4 * q4:4 * q4 + 4, :]
                    tt(sv, sv, Eb[:, 4 * q4:4 * q4 + 4, 4 * sq_ + 3:4 * sq_ + 4].to_broadcast([128, 4, 128]), MUL, [ks, "Eb"], [ks])
                    tt(sv, sv, bk[:, 0:512].rearrange("p (c n) -> p c n", c=4), ADD, [ks, pk], [ks])
                dma("sp", o_sssd[sq_].rearrange("(c p) n -> p c n", p=128), stb[s], [ks], ["out_sssd%d" % sq_])
            yo = A.alloc([128, 16, 16, 4], F32)
            for hb in range(2):
                cp(yo[:, 8 * hb:8 * hb + 8, :, :].rearrange("p s c t -> p (s c t)"), banks[yb[hb]][:, 0:512], [("ps", yb[hb])], ["yo"])
            reserved.clear()
            ysm = A.alloc([128, 16, 64], F32)
            ysm4 = ysm.rearrange("p c (s t) -> p c s t", t=4)
            tt(ysm4, yo.rearrange("p s c t -> p c s t"), Eb.rearrange("p c (s t) -> p c s t", t=4), MUL, ["yo", "Eb"], ["ysm"])
            tt(ysm, ysm, ydiag, ADD, ["ysm", "ydiag"], ["ysm"])
            ytmp = A.alloc([128, 16, 64], F32)
            tt(ytmp, xs_s, cols[:, C_DSK:C_DSK + 16].unsqueeze(2).to_broadcast([128, 16, 64]), MUL, ["xs_s", "cols"], ["ytmp"])
            tt(ysm, ysm, ytmp, ADD, ["ysm", "ytmp"], ["ysm"])
            uo_s = A.alloc([128, 16, 64], BF16)
            stt(uo_s, ysm, 0.5, zs_s, MUL, MUL, ["ysm", "zs_s"], ["uo_s"])
            dma("sp", u_dram[16:32, :, 1024:1088].rearrange("c p t -> p c t"), uo_s, ["uo_s"], [("u_dram", 99)])

        HALVES = [[(0, 128), (128, 128), (256, 128), (384, 128)], [(512, 128), (640, 128), (768, 128), (896, 128), (1024, 64)]]
        PH_OUT = PH - NM * 16
        if stage >= 6:
            S.barrier()
            A.off = PH_OUT
            ubuf = A.alloc([128, 32, NU], BF16)
            for c8 in range(4):
                dma("sp", ubuf[:, c8 * 8:(c8 + 1) * 8, :], u_dram[c8 * 8:(c8 + 1) * 8].rearrange("c p t -> p c t"), (),
                    [("ubuf", c) for c in range(c8 * 8, c8 * 8 + 8)])
            sqb = [A.alloc([128, NU], BF16) for _ in range(2)]
            rstd = A.alloc([128, NU], F32)
            groups = [(list(range(16)), 2048, C_GLRU)] + [([16 + 4 * g + j for j in range(4)], 512, C_GSSD + 4 * g) for g in range(4)]
            NT3 = [(0, 512), (512, 512), (1024, 64)]
            for gi, (chs, N, gc) in enumerate(groups):
                b3 = [nps() for _ in range(3)]
                for i, c in enumerate(chs):
                    s = i % 2
                    tt(sqb[s], ubuf[:, c, :], ubuf[:, c, :], MUL, [("ubuf", c)], [("sqb", s)])
                    for (bk, pk), (c0, n) in zip(b3, NT3):
                        mm(bk[:, 0:n], ones_b, sqb[s][:, c0:c0 + n], i == 0, i == len(chs) - 1, ["ones_b", ("sqb", s)], [pk])
                for (bk, pk), (c0, n) in zip(b3, NT3):
                    actf(rstd[:, c0:c0 + n], bk[:, 0:n], AF.Sqrt, [pk, "epsc"], ["rstd"], bias=epsc[:, 0:1], scale=1.0 / N)
                S.dve(lambda e: e.reciprocal(out=rstd, in_=rstd), ["rstd"], ["rstd"])
                for j, c in enumerate(chs):
                    stt(ubuf[:, c, :], ubuf[:, c, :], cols[:, gc + j:gc + j + 1], rstd, MUL, MUL, [("ubuf", c), "cols", "rstd"], [("ubuf", c)])
            xres = [A.alloc([128, 512], F32) for _ in range(3)]
            x1t = [A.alloc([128, 512], F32) for _ in range(3)]
            cnt = 0
            for hf in range(2):
                tiles = HALVES[hf]
                for fb in range(4):
                    bks = [nps() for _ in tiles]
                    for kq in range(4):
                        sl, sk = load_w(w_out[kq * 1024:(kq + 1) * 1024, fb * 512:(fb + 1) * 512].rearrange("(k p) n -> p k n", p=128), 8, 512)
                        for kk in range(8):
                            kc = kq * 8 + kk
                            for ti, (t0, nt) in enumerate(tiles):
                                mm(bks[ti][0][0:nt, :], ubuf[:, kc, t0:t0 + nt], sl[:, kk, :], kc == 0, kc == 31, [sk, ("ubuf", kc)], [bks[ti][1]])
                    for ti, (t0, nt) in enumerate(tiles):
                        i = cnt % 3
                        cnt += 1
                        dma("sp", xres[i][0:nt], xm[4 + t0:4 + t0 + nt, fb * 512:(fb + 1) * 512], (), [("xres", i)])
                        tt(x1t[i][0:nt], bks[ti][0][0:nt, :], xres[i][0:nt], ADD, [bks[ti][1], ("xres", i)], [("x1t", i)])
                        dma("sp", x1_dram[t0:t0 + nt, fb * 512:(fb + 1) * 512], x1t[i][0:nt], [("x1t", i)], [("x1d", hf, ti)])

        if stage >= 7:
            S.barrier()
            A.off = PH_OUT
            gfb = A.alloc([128, D], F32)
            dma("sp", gfb, g_final.partition_broadcast(128), (), ["gfb"])
            x1s = A.alloc([128, 5, D], F32)
            hpT = A.alloc([128, 16, 576], BF16)
            hbuf = A.alloc([128, 64, 576], BF16)
            rl = [A.alloc([128, 512], F32) for _ in range(2)]
            fj = A.alloc([128, D], BF16)
            fs = A.alloc([128, 2], F32)
            nb_mark = A.off
            ri = 0
            for hf in range(2):
                tiles = HALVES[hf]
                T0 = tiles[0][0]
                ntok = sum(nt for _, nt in tiles)
                for ti, (t0, nt) in enumerate(tiles):
                    dma("sp", x1s[0:nt, ti, :], x1_dram[t0:t0 + nt, :], [("x1d", hf, ti)], [("x1s", ti)])
                A.off = nb_mark
                hkeys = norm_T(None, ntok, C_GMLP, hpT, "hpT", pre=[(x1s[:, ti, :], ("x1s", ti)) for ti in range(len(tiles))])
                Ntiles = [(0, 512)] + ([(512, 64)] if hf == 1 else [])
                for jq in range(32):
                    sl, sk = load_w(w_up[:, jq * 256:(jq + 1) * 256].rearrange("(k p) n -> p k n", p=128), 16, 256)
                    for jj in range(2):
                        j = 2 * jq + jj
                        for (c0, n) in Ntiles:
                            bk, pk = nps()
                            for kc in range(16):
                                mm(bk[:, 0:n], sl[:, kc, jj * 128:(jj + 1) * 128], hpT[:, kc, c0:c0 + n], kc == 0, kc == 15,
                                   [sk] + tile_keys(hkeys, c0, n), [pk])
                            r = rl[ri % 2]
                            rk = ("rl", ri % 2)
                            ri += 1
                            actf(r[:, 0:n], bk[:, 0:n], AF.Relu, [pk], [rk])
                            tt(hbuf[:, j, c0:c0 + n], r[:, 0:n], r[:, 0:n], MUL, [rk], [("hbuf", j)])
                for fb in range(4):
                    bks = [nps() for _ in tiles]
                    for jq in range(8):
                        sl, sk = load_w(w_down[jq * 1024:(jq + 1) * 1024, fb * 512:(fb + 1) * 512].rearrange("(k p) n -> p k n", p=128), 8, 512)
                        for jj in range(8):
                            j = jq * 8 + jj
                            for ti, (t0, nt) in enumerate(tiles):
                                mm(bks[ti][0][0:nt, :], hbuf[:, j, t0 - T0:t0 - T0 + nt], sl[:, jj, :], j == 0, j == 63, [sk, ("hbuf", j)], [bks[ti][1]])
                    for ti, (t0, nt) in enumerate(tiles):
                        xv = x1s[0:nt, ti, fb * 512:(fb + 1) * 512]
                        tt(xv, bks[ti][0][0:nt, :], xv, ADD, [bks[ti][1], ("x1s", ti)], [("x1s", ti)])
                for ti, (t0, nt) in enumerate(tiles):
                    xv = x1s[0:nt, ti, :]
                    mset(fs[0:nt], 0.0, ["fs"])
                    actf(fj[0:nt], xv, AF.Square, [("x1s", ti), "fs"], ["fj", "fs"], accum=fs[0:nt, 0:1])
                    actf(fs[0:nt, 1:2], fs[0:nt, 0:1], AF.Sqrt, ["fs", "epsc"], ["fs"], bias=epsc[0:nt, 0:1], scale=1.0 / D)
                    S.dve(lambda e, nt=nt: e.reciprocal(out=fs[0:nt, 1:2], in_=fs[0:nt, 1:2]), ["fs"], ["fs"])
                    stt(xv, xv, fs[0:nt, 1:2], gfb[0:nt], MUL, MUL, [("x1s", ti), "fs", "gfb"], [("x1s", ti)])
                    dma("sp", y_out[t0:t0 + nt, :], xv, [("x1s", ti)], ["out_y_%d_%d" % (hf, ti)])

        print("arena high-water marks", ARENA_HW, "of", NAR)
        outs = [op.writes[0] for op in S.ops if op.dma and op.writes and isinstance(op.writes[0], str) and op.writes[0].startswith("out_")]
        S.add("sp", None, reads=outs)
        S.emit()
    return nc


def _colize(v):
    return np.ascontiguousarray(np.asarray(v, np.float32).reshape(-1, 128).T)


_NC_CACHE = {}


def kernel(x_prompt, x_sample, state_lru_h, state_lru_conv, state_ssd, state_ssd_conv, meta_tokens,
           g_mix, w_in, conv_lru_w, conv_lru_b, lru_wa, lru_ba, lru_wx, lru_bx, lru_lambda, g_lru_out,
           conv_ssd_w, conv_ssd_b, dt_bias, a_log, d_skip, g_ssd_out, w_out, g_mlp, w_up, w_down, g_final,
           _stage=99):
    f = lambda a: np.ascontiguousarray(np.asarray(a, dtype=np.float32))
    x_prompt, x_sample, meta_tokens = f(x_prompt), f(x_sample), f(meta_tokens)
    cols = np.zeros((128, NCOLS), np.float32)
    cols[:, C_GMIX:C_GMIX + 16] = _colize(g_mix[0])
    cols[:, C_GMLP:C_GMLP + 16] = _colize(g_mlp[0])
    for k in range(4):
        cols[:, C_CLW + k * 16:C_CLW + (k + 1) * 16] = _colize(conv_lru_w[0, k])
        cols[:, C_CSW + k * 24:C_CSW + (k + 1) * 24] = _colize(conv_ssd_w[0, k])
    cols[:, C_CLB:C_CLB + 16] = _colize(conv_lru_b[0])
    cols[:, C_BA:C_BA + 16] = _colize(lru_ba[0])
    cols[:, C_BX:C_BX + 16] = _colize(lru_bx[0])
    cols[:, C_LAM:C_LAM + 16] = _colize(lru_lambda[0])
    cols[:, C_GLRU:C_GLRU + 16] = _colize(g_lru_out[0])
    cols[:, C_CSB:C_CSB + 24] = _colize(conv_ssd_b[0])
    cols[:, C_GSSD:C_GSSD + 16] = _colize(g_ssd_out[0])
    cols[:, C_DSK:C_DSK + 16] = _colize(np.repeat(np.asarray(d_skip[0], np.float32), 64))
    cols[0:32, C_DTB] = np.asarray(dt_bias[0], np.float32)
    cols[0:32, C_ALOG] = np.asarray(a_log[0], np.float32)
    shared = dict(w_in=f(w_in[0]), lru_wa=f(lru_wa[0]), lru_wx=f(lru_wx[0]), w_out=f(w_out[0]), w_up=f(w_up[0]),
                  w_down=f(w_down[0]), g_final=f(g_final))
    in_maps = []
    for c in range(8):
        b, half = c // 2, c % 2
        cc = cols.copy()
        cc[:, C_FLAG] = float(half)
        xp = np.concatenate([meta_tokens, x_prompt[b, 0:1024]], axis=0)
        halo = meta_tokens[13:16] if half == 0 else x_prompt[b, 1021:1024]
        xm = np.concatenate([np.zeros((1, D), np.float32), halo, x_prompt[b, half * 1024:(half + 1) * 1024], x_sample[16 * c:16 * c + 16].reshape(64, D)], axis=0)
        m = dict(shared)
        m.update(xp=np.ascontiguousarray(xp), xm=np.ascontiguousarray(xm), cols=cc,
                 st_lru_h=f(state_lru_h[0, 16 * c:16 * c + 16]),
                 st_lru_conv=f(state_lru_conv[0, 16 * c:16 * c + 16]).reshape(48, D),
                 st_ssd=f(state_ssd[0, 16 * c:16 * c + 16]).reshape(16, 2048, 128),
                 st_ssd_conv=f(state_ssd_conv[0, 16 * c:16 * c + 16]).reshape(48, 3072))
        in_maps.append(m)
    if _stage not in _NC_CACHE:
        _NC_CACHE[_stage] = build(_stage)
    nc = _NC_CACHE[_stage]
    res = run_bass_kernel_spmd(nc, in_maps, core_ids=list(range(8)))
    R = res.results
    y_prompt = np.zeros((4, 2048, D), np.float32)
    y_sample = np.zeros((128, 4, D), np.float32)
    p_lru_h = np.zeros((1, 4, D), np.float32)
    p_lru_conv = np.zeros((1, 4, 3, D), np.float32)
    p_ssd = np.zeros((1, 4, 32, 64, 128), np.float32)
    p_ssd_conv = np.zeros((1, 4, 3, 3072), np.float32)
    s_lru_h = np.zeros((1, 128, D), np.float32)
    s_lru_conv = np.zeros((1, 128, 3, D), np.float32)
    s_ssd = np.zeros((1, 128, 32, 64, 128), np.float32)
    s_ssd_conv = np.zeros((1, 128, 3, 3072), np.float32)
    for c in range(8):
        b, half = c // 2, c % 2
        r = R[c]
        y_prompt[b, half * 1024:(half + 1) * 1024] = r["y_out"][0:1024]
        y_sample[16 * c:16 * c + 16] = r["y_out"][1024:1088].reshape(16, 4, D)
        tail = r["tail"]
        ts_ = tail[3:67].reshape(16, 4, 5120)[:, 1:4]
        s_lru_conv[0, 16 * c:16 * c + 16] = ts_[:, :, 0:2048]
        s_ssd_conv[0, 16 * c:16 * c + 16] = ts_[:, :, 2048:5120]
        s_lru_h[0, 16 * c:16 * c + 16] = r["s_lru_h"]
        s_ssd[0, 16 * c:16 * c + 16] = r["s_ssd"].reshape(16, 32, 64, 128)
        if half == 1:
            p_lru_h[0, b] = r["p_lru_h"]
            p_lru_conv[0, b] = tail[0:3, 0:2048]
            p_ssd_conv[0, b] = tail[0:3, 2048:5120]
            p_ssd[0, b] = r["p_ssd"].reshape(32, 64, 128)
    return (y_prompt, y_sample, p_lru_h, p_lru_conv, p_ssd, p_ssd_conv, s_lru_h, s_lru_conv, s_ssd, s_ssd_conv)
```

```python
import numpy as np
import concourse.bass as bass
import concourse.mybir as mybir
from concourse.bass_utils import run_bass_kernel_spmd
from contextlib import ExitStack

F32 = mybir.dt.float32
BF16 = mybir.dt.bfloat16
AF = mybir.ActivationFunctionType
ALU = mybir.AluOpType
AX = mybir.AxisListType


class _Op:
    __slots__ = ("eng", "fn", "reads", "writes", "dma", "deps", "waits", "sig", "sem", "semval", "idx", "barrier", "prev_wait")

    def __init__(self, eng, fn, reads, writes, dma):
        self.eng = eng
        self.fn = fn
        self.reads = tuple(reads)
        self.writes = tuple(writes)
        self.dma = dma
        self.waits = []
        self.sig = False
        self.sem = None
        self.semval = 0
        self.barrier = False
        self.prev_wait = None


class Sched:
    ENGS = ("pe", "act", "dve", "pool", "sp")
    SAME_ENG_DIST = 10 ** 9

    def __init__(self, nc, stack, n_dma_sems=None):
        self.nc = nc
        self.ops = []
        self.stack = stack
        nd = n_dma_sems or {"sp": 24, "act": 2, "pool": 12}
        self.csem = {e: stack.enter_context(nc.semaphore("cs_" + e)) for e in ("pe", "act", "dve", "pool")}
        self.dsem = {q: [stack.enter_context(nc.semaphore("ds_%s_%d" % (q, i))) for i in range(n)]
                     for q, n in nd.items()}

    def add(self, eng, fn, reads=(), writes=(), dma=False):
        op = _Op(eng, fn, reads, writes, dma)
        op.idx = len(self.ops)
        self.ops.append(op)
        return op

    def pe(self, fn, r=(), w=()): return self.add("pe", fn, r, w)
    def act(self, fn, r=(), w=()): return self.add("act", fn, r, w)
    def dve(self, fn, r=(), w=()): return self.add("dve", fn, r, w)
    def pool(self, fn, r=(), w=()): return self.add("pool", fn, r, w)
    def dma(self, q, fn, r=(), w=()): return self.add(q, fn, r, w, dma=True)

    def barrier(self):
        for e in self.ENGS:
            op = self.add(e, None)
            op.barrier = True

    def finalize(self):
        ops = self.ops
        last_writer = {}
        readers = {}
        last_real = {}
        dmas_since = []
        for op in ops:
            if op.barrier:
                deps = {}
                for e, i in last_real.items():
                    if e != op.eng:
                        deps[i] = True
                for i in dmas_since:
                    deps[i] = True
                op.deps = deps
                if op.eng == self.ENGS[-1]:
                    dmas_since = []
                continue
            if op.fn is not None:
                if op.dma:
                    dmas_since.append(op.idx)
                else:
                    last_real[op.eng] = op.idx
            deps = {}
            for k in op.reads:
                lw = last_writer.get(k)
                if lw is not None:
                    deps[lw] = True
            for k in op.writes:
                lw = last_writer.get(k)
                if lw is not None and lw not in deps:
                    deps[lw] = deps.get(lw, False)
                for r in readers.get(k, ()):
                    if r not in deps:
                        deps[r] = False
            deps.pop(op.idx, None)
            op.deps = deps
            for k in op.reads:
                lst = readers.setdefault(k, [])
                if not op.dma:
                    lst[:] = [r for r in lst if ops[r].dma or ops[r].eng != op.eng]
                lst.append(op.idx)
            for k in op.writes:
                last_writer[k] = op.idx
                readers[k] = []
        need = []
        pos = {}
        cnt = {e: 0 for e in self.ENGS}
        for op in ops:
            if op.fn is not None and not op.dma:
                cnt[op.eng] += 1
            pos[op.idx] = cnt[op.eng]
        for op in ops:
            lst = []
            for d, raw in op.deps.items():
                dop = ops[d]
                if dop.dma:
                    lst.append(d)
                elif dop.eng == op.eng:
                    if op.dma:
                        lst.append(d)
                    elif op.eng == "pe":
                        continue
                    else:
                        lst.append(d)
                else:
                    lst.append(d)
            need.append(lst)
            for d in lst:
                ops[d].sig = True
        ccount = {e: 0 for e in self.csem}
        dcount = {q: [0] * len(v) for q, v in self.dsem.items()}
        drr = {q: 0 for q in self.dsem}
        for op in ops:
            if op.dma:
                q = op.eng
                i = drr[q] % len(self.dsem[q])
                drr[q] += 1
                op.prev_wait = (self.dsem[q][i], dcount[q][i]) if dcount[q][i] > 0 else None
                dcount[q][i] += 16
                op.sem = self.dsem[q][i]
                op.semval = dcount[q][i]
                op.sig = True
            elif op.sig and op.fn is not None:
                ccount[op.eng] += 1
                op.sem = self.csem[op.eng]
                op.semval = ccount[op.eng]
        waited = {e: {} for e in self.ENGS}
        for op, lst in zip(ops, need):
            w = {}
            pw = getattr(op, "prev_wait", None)
            if pw is not None:
                w[id(pw[0])] = pw
            for d in lst:
                dop = ops[d]
                key = id(dop.sem)
                if w.get(key, (None, 0))[1] < dop.semval:
                    w[key] = (dop.sem, dop.semval)
            wl = []
            wd = waited[op.eng]
            for key, (sem, val) in w.items():
                if wd.get(key, 0) < val:
                    wd[key] = val
                    wl.append((sem, val))
            op.waits = wl
        self.n_ops = len(ops)

    def emit(self):
        nc = self.nc
        self.finalize()
        per = {e: [op for op in self.ops if op.eng == e] for e in self.ENGS}

        def run(eng, lst):
            for op in lst:
                for sem, val in op.waits:
                    eng.wait_ge(sem, val)
                if op.fn is not None:
                    inst = op.fn(eng)
                    if op.sig:
                        inst.then_inc(op.sem, 16 if op.dma else 1)

        with nc.Block() as block:
            @block.sync
            def _(e):
                run(e, per["sp"])

            @block.scalar
            def _(e):
                run(e, per["act"])

            @block.vector
            def _(e):
                run(e, per["dve"])

            @block.gpsimd
            def _(e):
                run(e, per["pool"])

            @block.tensor
            def _(e):
                run(e, per["pe"])


D = 2048
KC = 16
NPRE = 1040
NM = 1092
NU = 1088
O1, O2, O3, O4 = 2048, 4096, 6144, 9216
EPS = 1e-6
NEG = -30000.0

C_GMIX, C_GMLP, C_CLW, C_CLB, C_BA, C_BX, C_LAM, C_GLRU = 0, 16, 32, 96, 112, 128, 144, 160
C_CSW, C_CSB, C_GSSD, C_DSK, C_DTB, C_ALOG, C_FLAG, NCOLS = 176, 272, 296, 312, 328, 329, 330, 336


def build(stage=99):
    nc = bass.Bass("TRN2", target_bir_lowering=False)

    def din(name, shape, dt=F32):
        return nc.dram_tensor(name, list(shape), dt, kind="ExternalInput").ap()

    def dout(name, shape, dt=F32):
        return nc.dram_tensor(name, list(shape), dt, kind="ExternalOutput").ap()

    def dscr(name, shape, dt=F32):
        return nc.dram_tensor(name, list(shape), dt).ap()

    xp = din("xp", [NPRE, D])
    xm = din("xm", [NM, D])
    cols_d = din("cols", [128, NCOLS])
    st_h = din("st_lru_h", [16, D])
    st_lc = din("st_lru_conv", [48, D])
    st_ssd = din("st_ssd", [16, 2048, 128])
    st_sc = din("st_ssd_conv", [48, 3072])
    w_in = din("w_in", [D, 9248])
    wa_d = din("lru_wa", [16, 128, 128])
    wx_d = din("lru_wx", [16, 128, 128])
    w_out = din("w_out", [4096, D])
    w_up = din("w_up", [D, 8192])
    w_down = din("w_down", [8192, D])
    g_final = din("g_final", [D])

    y_out = dout("y_out", [NU, D])
    o_plh = dout("p_lru_h", [D])
    o_tail = dout("tail", [67, 5120])
    o_pssd = dout("p_ssd", [2048, 128])
    o_slh = dout("s_lru_h", [16, D])
    o_sssd = dout("s_ssd", [16, 2048, 128])

    u_dram = dscr("u_dram", [32, 128, NU], BF16)
    ac_dram = dscr("ac_dram", [32, NM])
    x1_dram = dscr("x1_dram", [NU, D])

    with ExitStack() as st:
        S = Sched(nc, st)
        NAR = 105600
        arena_t = st.enter_context(nc.sbuf_tensor("arena", [128, NAR], BF16))
        banks = [st.enter_context(nc.psum_tensor("ps%d" % i, [128, 512], F32)) for i in range(8)]

        ARENA_HW = [0]

        class Arena:
            def __init__(self):
                self.off = 0

            def alloc(self, shape, dt):
                n = int(np.prod(shape[1:]))
                nb = n * (2 if dt == F32 else 1)
                off = (self.off + 31) // 32 * 32
                assert off + nb <= NAR, ("arena overflow", off, nb)
                v = arena_t[:, off:off + nb]
                if dt == F32:
                    v = v.bitcast(F32)
                if shape[0] < 128:
                    v = v[0:shape[0]]
                if len(shape) == 3:
                    v = v.rearrange("p (a b) -> p a b", a=shape[1])
                elif len(shape) == 4:
                    v = v.rearrange("p (a b c) -> p a b c", a=shape[1], b=shape[2])
                self.off = off + nb
                ARENA_HW[0] = max(ARENA_HW[0], self.off)
                return v

        A = Arena()
        psi = [0]

        def nps():
            b = psi[0] % 8
            psi[0] += 1
            return banks[b], ("ps", b)

        def bf_bank(bk):
            return bk.bitcast(BF16)

        def tt(out, a, b, op, r, w, eng="dve"):
            S.add(eng, lambda e: e.tensor_tensor(out=out, in0=a, in1=b, op=op), r, w)

        def ts(out, a, s1, s2, op0, op1, r, w, eng="dve"):
            if s2 is None:
                S.add(eng, lambda e: e.tensor_scalar(out=out, in0=a, scalar1=s1, scalar2=None, op0=op0), r, w)
            else:
                S.add(eng, lambda e: e.tensor_scalar(out=out, in0=a, scalar1=s1, scalar2=s2, op0=op0, op1=op1), r, w)

        def stt(out, a, s, b, op0, op1, r, w, eng="dve"):
            S.add(eng, lambda e: e.scalar_tensor_tensor(out=out, in0=a, scalar=s, in1=b, op0=op0, op1=op1), r, w)

        def actf(out, in_, func, r, w, bias=None, scale=None, accum=None):
            kw = {}
            if bias is not None:
                kw["bias"] = bias
            if scale is not None:
                kw["scale"] = scale
            if accum is not None:
                kw["accum_out"] = accum
            S.add("act", lambda e: e.activation(out=out, in_=in_, func=func, **kw), r, w)

        def cp(out, in_, r, w, eng="dve"):
            if eng == "act":
                S.add("act", lambda e: e.copy(out=out, in_=in_), r, w)
            else:
                S.add(eng, lambda e: e.tensor_copy(out=out, in_=in_), r, w)

        def mset(ap, val, w, eng="dve"):
            S.add(eng, lambda e: e.memset(ap, val), (), w)

        def mm(out, lhsT, rhs, start, stop, r, w):
            S.add("pe", lambda e: e.matmul(out, lhsT=lhsT, rhs=rhs, start=start, stop=stop), r, w)

        def tr(out, in_, ident, r, w):
            S.add("pe", lambda e: e.transpose(out=out, in_=in_, identity=ident), r, w)

        def dma(q, out, in_, r, w):
            S.add(q, lambda e: e.dma_start(out=out, in_=in_), r, w, dma=True)

        def dma_nc(q, out, in_, r, w):
            S.add(q, lambda e: e.dma_start(out=out, in_=in_, allow_slow_non_contiguous=True), r, w, dma=True)

        MUL, ADD, SUB, MAX, MIN, POW = ALU.mult, ALU.add, ALU.subtract, ALU.max, ALU.min, ALU.pow

        cols = A.alloc([128, NCOLS], F32)
        ident_b = A.alloc([128, 128], BF16)
        ident_f = A.alloc([128, 128], F32)
        ones_b = A.alloc([128, 128], BF16)
        ones_f = A.alloc([128, 128], F32)
        negmask = A.alloc([128, 128], F32)
        negmask_s = A.alloc([64, 64], F32)
        cneg = A.alloc([128, 16], F32)
        acol = A.alloc([32, 1], F32)
        h0col = A.alloc([128, 16], F32)
        hmeta = A.alloc([128, 16], F32)
        hT = A.alloc([128, 2048], F32)
        hTm = A.alloc([128, 2048], F32)
        tmpc = A.alloc([128, 16], F32)
        epsc = A.alloc([128, 1], F32)
        S.pool(lambda e: e.memset(epsc, EPS), (), ["epsc"])

        dma("sp", cols, cols_d, (), ["cols"])
        for t_, key in ((ident_b, "ident_b"), (ident_f, "ident_f")):
            S.pool(lambda e, t_=t_: e.memset(t_, 1.0), (), [key])
            S.pool(lambda e, t_=t_: e.affine_select(out=t_, in_=t_, pattern=[[-1, 128]], compare_op=ALU.is_equal,
                                                    fill=0.0, base=0, channel_multiplier=1), [key], [key])
        S.pool(lambda e: e.memset(ones_b, 1.0), (), ["ones_b"])
        S.pool(lambda e: e.memset(ones_f, 1.0), (), ["ones_f"])
        S.pool(lambda e: e.memset(negmask, 0.0), (), ["negmask"])
        S.pool(lambda e: e.affine_select(out=negmask, in_=negmask, pattern=[[1, 128]], compare_op=ALU.is_ge,
                                         fill=NEG, base=0, channel_multiplier=-1), ["negmask"], ["negmask"])
        S.pool(lambda e: e.memset(negmask_s, 0.0), (), ["negmask_s"])
        nms3 = negmask_s.rearrange("p (s t) -> p s t", t=4)
        S.pool(lambda e: e.affine_select(out=nms3, in_=nms3, pattern=[[4, 16], [1, 4]], compare_op=ALU.is_ge,
                                         fill=NEG, base=0, channel_multiplier=-1), ["negmask_s"], ["negmask_s"])
        S.pool(lambda e: e.affine_select(out=nms3, in_=nms3, pattern=[[-4, 16], [0, 4]], compare_op=ALU.is_ge,
                                         fill=NEG, base=0, channel_multiplier=1), ["negmask_s"], ["negmask_s"])
        actf(cneg, cols[:, C_LAM:C_LAM + 16], AF.Exp, ["cols"], ["cneg"], scale=-1.0)
        actf(cneg, cneg, AF.Ln, ["cneg"], ["cneg"], bias=1.0, scale=1.0)
        ts(cneg, cneg, -8.0, None, MUL, None, ["cneg"], ["cneg"])
        actf(acol, cols[0:32, C_ALOG:C_ALOG + 1], AF.Exp, ["cols"], ["acol"])
        ts(acol, acol, -1.0, None, MUL, None, ["acol"], ["acol"])
        A_small = dict(hba=A.alloc([128, 16], F32), hbx=A.alloc([128, 16], F32), hc=A.alloc([128, 16], F32))
        ts(A_small["hba"], cols[:, C_BA:C_BA + 16], 0.5, None, MUL, None, ["cols"], ["hb"])
        ts(A_small["hbx"], cols[:, C_BX:C_BX + 16], 0.5, None, MUL, None, ["cols"], ["hb"])
        ts(A_small["hc"], cneg, 0.5, None, MUL, None, ["cneg"], ["hb"])
        csh = A.alloc([128, 120], F32)
        ts(csh, cols[:, C_CSW:C_CSW + 120], 0.5, None, MUL, None, ["cols"], ["csh"])
        base_off = A.off

        NSLAB = 3
        wslab = [A.alloc([128, 4096], BF16) for _ in range(NSLAB)]
        wsi = [0]

        def load_w(src_ap_3d, kc, ncol):
            i = wsi[0] % NSLAB
            wsi[0] += 1
            v = wslab[i][:, 0:kc * ncol].rearrange("p (k n) -> p k n", k=kc)
            key = ("wslab", i)
            dma("pool", v, src_ap_3d, (), [key])
            return v, key

        def w_in_cols(c0, ncol):
            return w_in[:, c0:c0 + ncol].rearrange("(k p) n -> p k n", p=128)

        reserved = set()

        def nps():
            while True:
                b = psi[0] % 8
                psi[0] += 1
                if b not in reserved:
                    return banks[b], ("ps", b)

        xnT = A.alloc([128, 16, NM], BF16)
        PH = A.off

        def norm_T(src, n_tok, gc0, dstT, dkey, pre=None, c0T=0):
            xt = [A.alloc([128, D], F32) for _ in range(2)] if pre is None else None
            xb = [A.alloc([128, D], BF16) for _ in range(2)]
            ss = [A.alloc([128, 2], F32) for _ in range(2)]
            for ti, r0 in enumerate(range(0, n_tok, 128)):
                n = min(128, n_tok - r0)
                s = ti % 2
                kb, ks = ("nxb", s), ("nss", s)
                if pre is None:
                    kx = ("nxt", s)
                    xin = xt[s][0:n]
                    dma("sp", xin, src[r0:r0 + n, :], (), [kx])
                else:
                    xin, kx = pre[ti][0][0:n], pre[ti][1]
                mset(ss[s][0:n], 0.0, [ks])
                actf(xb[s][0:n], xin, AF.Square, [kx, ks], [kb, ks], accum=ss[s][0:n, 0:1])
                actf(ss[s][0:n, 1:2], ss[s][0:n, 0:1], AF.Sqrt, [ks, "epsc"], [ks], bias=epsc[0:n, 0:1], scale=1.0 / D)
                S.dve(lambda e, s=s, n=n: e.reciprocal(out=ss[s][0:n, 1:2], in_=ss[s][0:n, 1:2]), [ks], [ks])
                ts(xb[s][0:n], xin, ss[s][0:n, 1:2], None, MUL, None, [kx, ks, kb], [kb])
                for half in range(2):
                    bk, pk = nps()
                    pst = bf_bank(bk).rearrange("p (a b) -> p a b", a=8)
                    for q in range(8):
                        kc = half * 8 + q
                        tr(pst[:, q, 0:n], xb[s][0:n, kc * 128:(kc + 1) * 128], ident_b[0:n, 0:n], [kb, "ident_b"], [pk])
                    gbc = cols[:, gc0 + half * 8:gc0 + half * 8 + 8].unsqueeze(2).to_broadcast([128, 8, n])
                    tt(dstT[:, half * 8:half * 8 + 8, c0T + r0:c0T + r0 + n], pst[:, :, 0:n], gbc, MUL, [pk, "cols"], [(dkey, ti)])
            return [(dkey, ti) for ti in range((n_tok + 127) // 128)]

        def tile_keys(keys, c0, n):
            return [keys[t] for t in range(c0 // 128, (c0 + n - 1) // 128 + 1)]

        def conv4(src, wcol, bcol, out, rkeys, wkey, tab=None, tkey="cols"):
            tab = cols if tab is None else tab
            ts(out, src(0), tab[:, wcol(0):wcol(0) + 1], tab[:, bcol:bcol + 1], MUL, ADD, rkeys + [tkey], [wkey])
            for k in range(1, 4):
                stt(out, src(k), tab[:, wcol(k):wcol(k) + 1], out, MUL, ADD, rkeys + [tkey, wkey], [wkey])

        DGS = {}

        def get_diag(tag, wcols):
            if DGS.get("tag") == tag:
                return DGS["ap"], DGS["key"]
            i = DGS.get("i", 0)
            DGS["i"] = i + 1
            dg = DGS["bufs"][i % 2]
            key = ("dg", i % 2)
            for k in range(4):
                ts(dg[:, k, :], ident_b, cols[:, wcols[k]:wcols[k] + 1], None, MUL, None, ["ident_b", "cols"], [key])
            DGS.update(tag=tag, ap=dg, key=key)
            return dg, key

        NLT = 6
        L = {}

        def lru_setup():
            def aset(w):
                return dict(xl=A.alloc([128, w], F32), xlb=A.alloc([128, w], BF16), ra=A.alloc([128, w], F32),
                            ii=A.alloc([128, w], F32), m=A.alloc([128, w], F32), gs=A.alloc([128, w], BF16))

            def bset(w):
                return dict(hs=A.alloc([128, w], F32), gl=A.alloc([128, w], F32), uo=A.alloc([128, w], BF16))
            L["A"] = {"big": [aset(512) for _ in range(4)], "small": [aset(64) for _ in range(2)]}
            L["B"] = {"big": [bset(512) for _ in range(2)], "small": [bset(64) for _ in range(2)]}
            L["ai"] = {"big": 0, "small": 0}
            L["bi"] = {"big": 0, "small": 0}
            L["lt"] = object()
            L["wab"] = A.alloc([128, 16, 128], BF16)
            L["wxb"] = A.alloc([128, 16, 128], BF16)
            dma("pool", L["wab"], wa_d.rearrange("h i j -> i h j"), (), ["wab"])
            dma("pool", L["wxb"], wx_d.rearrange("h i j -> i h j"), (), ["wxb"])
            L["uraw"] = [A.alloc([128, 4 + NM], BF16) for _ in range(2)]
            for i in range(2):
                mset(L["uraw"][i][:, 0:3], 0.0, [("uraw", i, -1)])
            DGS["bufs"] = [A.alloc([128, 4, 128], BF16) for _ in range(2)]
            DGS["tag"] = None

        hba = A_small["hba"]
        hbx = A_small["hbx"]
        hc = A_small["hc"]

        def lru_A(h, n, conv_src, conv_keys, sample=False):
            pool = "small" if n <= 64 else "big"
            s = L["ai"][pool] % len(L["A"][pool])
            L["ai"][pool] += 1
            T = L["A"][pool][s]
            wab, wxb = L["wab"], L["wxb"]
            K = lambda nm: ("ltA", pool, nm, s)

            def v(ap):
                a = ap[:, 0:n]
                return a.rearrange("p (s t) -> p s t", t=4) if sample else a
            xl, xlb, ra, ii, m = (T[k] for k in ("xl", "xlb", "ra", "ii", "m"))
            loc = {}

            def s1():
                dg, dgk = get_diag(("lru", h, id(L["lt"])), [C_CLW + k * 16 + h for k in range(4)])
                bk, pk = nps()
                for k in range(4):
                    mm(v(bk), dg[:, k, :], conv_src(k), k == 0, k == 3, conv_keys + [dgk], [pk])
                bcol = cols[:, C_CLB + h:C_CLB + h + 1]
                actf(v(xl), v(bk), AF.Identity, [pk, "cols"], [K("xl")], bias=bcol, scale=1.0)
                actf(v(xlb), v(bk), AF.Identity, [pk, "cols"], [K("xlb")], bias=bcol, scale=1.0)

            def s2():
                pass

            def s3():
                bk1, pk1 = nps()
                mm(bk1[:, 0:n], wab[:, h, :], xlb[:, 0:n], True, True, ["wab", K("xlb")], [pk1])
                bk2, pk2 = nps()
                mm(bk2[:, 0:n], wxb[:, h, :], xlb[:, 0:n], True, True, ["wxb", K("xlb")], [pk2])
                loc.update(bk1=bk1, pk1=pk1, bk2=bk2, pk2=pk2)

            def s4():
                actf(ra[:, 0:n], loc["bk1"][:, 0:n], AF.Tanh, [loc["pk1"], "hb"], [K("ra")], bias=hba[:, h:h + 1], scale=0.5)
                actf(ii[:, 0:n], loc["bk2"][:, 0:n], AF.Tanh, [loc["pk2"], "hb"], [K("ii")], bias=hbx[:, h:h + 1], scale=0.5)
                actf(ra[:, 0:n], ra[:, 0:n], AF.Exp, [K("ra"), "hb"], [K("ra")], bias=hc[:, h:h + 1], scale=hc[:, h:h + 1])

            def s5():
                tt(m[:, 0:n], ra[:, 0:n], ra[:, 0:n], MUL, [K("ra")], [K("m")])
            def s34():
                s3()
                s4()
            return dict(s=s, T=T, pool=pool, h=h, n=n, sample=sample, v=v, K=K, stages=[s1, s2, s34, s5])

        def run_stages(sts):
            for k in range(4):
                for st_ in sts:
                    st_["stages"][k]()

        def lru_sqrt(st, first):
            T = st["T"]
            m, n, K = T["m"], st["n"], st["K"]
            actf(m[:, 0:n], m[:, 0:n], AF.Sqrt, [K("m")], [K("m")], bias=1.0, scale=-1.0)
            if first:
                mset(m[:, 0:1], 1.0, [K("m")])

        def lru_B(st, init, init_keys, gate_ps=None, gate_key=None, h0s=None):
            pool = st["pool"]
            bs_ = L["bi"][pool] % len(L["B"][pool])
            L["bi"][pool] += 1
            T = dict(st["T"])
            T.update(L["B"][pool][bs_])
            n, KA, v, sample = st["n"], st["K"], st["v"], st["sample"]
            K = lambda nm: ("ltB", pool, nm, bs_) if nm in ("hs", "gl", "uo") else KA(nm)
            xl, ra, ii, m, hs, gl, uo = (T[k] for k in ("xl", "ra", "ii", "m", "hs", "gl", "uo"))
            stt(m[:, 0:n], ii[:, 0:n], 1.0, m[:, 0:n], ADD, MUL, [K("m"), K("ii")], [K("m")])
            stt(m[:, 0:n], m[:, 0:n], 0.5, xl[:, 0:n], MUL, MUL, [K("m"), K("xl")], [K("m")])
            if not sample:
                S.dve(lambda e: e.tensor_tensor_scan(out=hs[:, 0:n], data0=ra[:, 0:n], data1=m[:, 0:n], initial=init,
                                                     op0=MUL, op1=ADD), [K("ra"), K("m")] + init_keys, [K("hs")])
            else:
                a3, b3, h3 = v(ra), v(m), v(hs)
                for t_ in range(4):
                    prev = h0s if t_ == 0 else h3[:, :, t_ - 1]
                    tt(h3[:, :, t_], a3[:, :, t_], prev, MUL, [K("ra"), K("hs")] + init_keys, [K("hs")])
                    tt(h3[:, :, t_], h3[:, :, t_], b3[:, :, t_], ADD, [K("hs"), K("m")], [K("hs")])
            if gate_ps is not None:
                actf(gl[:, 0:n], gate_ps, AF.Square, [gate_key], [K("gl")])
                ts(gl[:, 0:n], gl[:, 0:n], 0.044715, 1.0, MUL, ADD, [K("gl")], [K("gl")])
                tt(gl[:, 0:n], gl[:, 0:n], gate_ps, MUL, [K("gl"), gate_key], [K("gl")])
                actf(gl[:, 0:n], gl[:, 0:n], AF.Tanh, [K("gl")], [K("gl")], scale=0.7978845608)
                stt(gl[:, 0:n], gl[:, 0:n], 1.0, gate_ps, ADD, MUL, [K("gl"), gate_key], [K("gl")])
                stt(uo[:, 0:n], gl[:, 0:n], 0.5, hs[:, 0:n], MUL, MUL, [K("gl"), K("hs")], [K("uo")])
            return hs, K("hs"), uo, K("uo")

        PT = [(0, 16), (16, 512), (528, 512)]
        PCH = [(0, 16)] + [(16 + 128 * k, 128) for k in range(8)]
        MT = [(4, 512), (516, 512)]
        MCH = [(4 + 128 * k, 128) for k in range(8)]
        ST = (1028, 64)

        A.off = PH
        pre_keys = norm_T(xp, NPRE, C_GMIX, xnT, "xnTp")
        lru_setup()
        def pre_B(h, sts):
            for ti in range(3):
                lru_sqrt(sts[ti], ti == 0)
            prev_hs, prev_key = None, None
            for ti, (c0, n) in enumerate(PT):
                init = 0.0 if ti == 0 else prev_hs[:, PT[ti - 1][1] - 1:PT[ti - 1][1]]
                ikeys = [] if ti == 0 else [prev_key]
                hs, hk, _, _ = lru_B(sts[ti], init, ikeys)
                if ti == 0:
                    cp(hmeta[:, h:h + 1], hs[:, n - 1:n], [hk], ["hmeta"])
                prev_hs, prev_key = hs, hk
            cp(h0col[:, h:h + 1], prev_hs[:, 511:512], [prev_key], ["h0col"])

        pend = None
        for hq in range(8):
            xs_, xk = load_w(w_in_cols(O1 + hq * 256, 256), 16, 256)
            for hh in range(2):
                h = hq * 2 + hh
                ub = h % 2
                ur = L["uraw"][ub]
                sts = []
                for ti, (c0, n) in enumerate(PT):
                    bk, pk = nps()
                    for kc in range(16):
                        mm(bk[:, 0:n], xs_[:, kc, hh * 128:(hh + 1) * 128], xnT[:, kc, c0:c0 + n], kc == 0, kc == 15,
                           [xk] + tile_keys(pre_keys, c0, n), [pk])
                    cp(ur[:, 3 + c0:3 + c0 + n], bk[:, 0:n], [pk], [("uraw", ub, ti)], eng="act")
                    sts.append(lru_A(h, n, lambda k, ur=ur, c0=c0, n=n: ur[:, c0 + k:c0 + k + n], [("uraw", ub, ti), ("uraw", ub, ti - 1)]))
                run_stages(sts)
                if pend is not None:
                    pre_B(*pend)
                pend = (h, sts)
        pre_B(*pend)
        tt(tmpc, h0col, hmeta, SUB, ["h0col", "hmeta"], ["tmpc"])
        stt(h0col, tmpc, cols[:, C_FLAG:C_FLAG + 1], hmeta, MUL, ADD, ["tmpc", "cols", "hmeta"], ["h0col"])

        Z = {}

        def ssd_setup(main):
            Z["dtT"] = A.alloc([32, NM], F32)
            Z["dA"] = A.alloc([32, NM], F32)
            Z["acum"] = A.alloc([32, NM], F32)
            Z["decs"] = A.alloc([32, 16], F32)
            Z["dg"] = [A.alloc([32, 32], F32) for _ in range(2)]
            Z["decbc"] = A.alloc([128, 10, 32], F32)
            Z["tok"] = A.alloc([128, 10, 96], F32)
            if main:
                Z["xs_s"] = A.alloc([128, 16, 64], BF16)
                Z["zs_s"] = A.alloc([128, 16, 64], BF16)
                Z["Bs_s"] = A.alloc([128, 4, 64], BF16)
                Z["Cs_s"] = A.alloc([128, 4, 64], BF16)
                Z["mark_s"] = A.off
            Z["xs"] = [A.alloc([128, 4, NM], BF16) for _ in range(2)]
            Z["Bs"] = [A.alloc([128, NM], BF16) for _ in range(2)]
            Z["uraw"] = [A.alloc([128, 4 + NM], BF16) for _ in range(2)]
            for i in range(2):
                mset(Z["uraw"][i][:, 0:3], 0.0, [("uraw", i, -1)])
            DGS["bufs"] = [A.alloc([128, 4, 128], BF16) for _ in range(2)]
            DGS["tag"] = None
            Z["uri"] = 0
            Z["ctmp"] = [A.alloc([128, 512], F32) for _ in range(3)]
            Z["cti"] = 0
            Z["xw"] = [A.alloc([128, 512], BF16) for _ in range(2)]
            Z["Bt"] = [A.alloc([128, 128], BF16) for _ in range(2)]
            Z["stmp"] = [A.alloc([128, 512], F32) for _ in range(1)]
            if main:
                Z["Cs"] = [A.alloc([128, NM], BF16) for _ in range(2)]
                Z["zs"] = [A.alloc([128, 4, NU], BF16) for _ in range(2)]
                Z["hTb"] = A.alloc([128, 2048], BF16)
                Z["Rb"] = [A.alloc([128, 8, 128], F32) for _ in range(2)]
                Z["Eo"] = [A.alloc([128, 8, 128], BF16) for _ in range(2)]
                Z["Lm"] = [A.alloc([128, 8, 128], BF16) for _ in range(2)]
                Z["Csc"] = [A.alloc([128, 8, 128], BF16) for _ in range(2)]
                Z["cbT"] = [A.alloc([128, 128], BF16) for _ in range(2)]
                Z["xdt"] = [A.alloc([128, 512], BF16) for _ in range(2)]
                Z["yt"] = [A.alloc([128, 4, 128], F32) for _ in range(2)]
                Z["uo"] = [A.alloc([128, 4, 128], BF16) for _ in range(2)]
                Z["scT"] = A.alloc([128, 24, 48], F32)
                Z["uext"] = [A.alloc([128, 16, 8], BF16) for _ in range(2)]
                Z["tailst"] = [A.alloc([67, 256], F32) for _ in range(2)]
                Z["tsi"] = 0

        def ssd_dt(xkeys, tiles, chunks, main):
            dtT, dA, acum, decs, dg, decbc, tok = (Z[k] for k in ("dtT", "dA", "acum", "decs", "dg", "decbc", "tok"))
            ds_, dk = load_w(w_in_cols(O4, 32), 16, 32)
            for (c0, n) in tiles:
                bk, pk = nps()
                for kc in range(16):
                    mm(bk[0:32, 0:n], ds_[:, kc, :], xnT[:, kc, c0:c0 + n], kc == 0, kc == 15, [dk] + tile_keys(xkeys, c0, n), [pk])
                actf(dtT[:, c0:c0 + n], bk[0:32, 0:n], AF.Exp, [pk, "cols"], ["dtT"], bias=cols[0:32, C_DTB:C_DTB + 1], scale=1.0)
                actf(dtT[:, c0:c0 + n], dtT[:, c0:c0 + n], AF.Ln, ["dtT"], ["dtT"], bias=1.0, scale=1.0)
                ts(dA[:, c0:c0 + n], dtT[:, c0:c0 + n], acol[:, 0:1], None, MUL, None, ["dtT", "acol"], ["dA"])
            for ci, (c0, q) in enumerate(chunks):
                S.dve(lambda e, c0=c0, q=q: e.tensor_tensor_scan(out=acum[:, c0:c0 + q], data0=ones_f[0:32, 0:q], data1=dA[:, c0:c0 + q],
                                                                 initial=0.0, op0=MUL, op1=ADD), ["ones_f", "dA"], ["acum"])
                last = acum[:, c0 + q - 1:c0 + q]
                actf(dA[:, c0:c0 + q], acum[:, c0:c0 + q], AF.Exp, ["acum", "dA"], ["dA"], bias=last, scale=-1.0)
                tt(dA[:, c0:c0 + q], dA[:, c0:c0 + q], dtT[:, c0:c0 + q], MUL, ["dA", "dtT"], ["dA"])
                actf(decs[:, ci:ci + 1], last, AF.Exp, ["acum"], ["decs"])
                ts(dg[ci % 2], ident_f[0:32, 0:32], decs[:, ci:ci + 1], None, MUL, None, ["ident_f", "decs"], [("dg", ci % 2)])
                bk, pk = nps()
                mm(bk[:, 0:32], ones_f[0:32, :], dg[ci % 2], True, True, ["ones_f", ("dg", ci % 2)], [pk])
                cp(decbc[:, ci, :], bk[:, 0:32], [pk], ["decbc"], eng="act")
                bk, pk = nps()
                tr(bk[0:q, 0:32], dtT[:, c0:c0 + q], ident_f[0:32, 0:32], ["dtT", "ident_f"], [pk])
                tr(bk[0:q, 32:64], dA[:, c0:c0 + q], ident_f[0:32, 0:32], ["dA", "ident_f"], [pk])
                tr(bk[0:q, 64:96], acum[:, c0:c0 + q], ident_f[0:32, 0:32], ["acum", "ident_f"], [pk])
                cp(tok[0:q, ci, :], bk[0:q, 0:96], [pk], ["tok"])
            if main:
                c0 = ST[0]
                a3 = acum[:, c0:c0 + 64].rearrange("p (s t) -> p s t", t=4)
                d3 = dA[:, c0:c0 + 64].rearrange("p (s t) -> p s t", t=4)
                t3 = dtT[:, c0:c0 + 64].rearrange("p (s t) -> p s t", t=4)
                cp(a3[:, :, 0], d3[:, :, 0], ["dA"], ["acum"])
                for t in range(1, 4):
                    tt(a3[:, :, t], a3[:, :, t - 1], d3[:, :, t], ADD, ["acum", "dA"], ["acum"])
                tt(d3, a3[:, :, 3:4].to_broadcast([32, 16, 4]), a3, SUB, ["acum", "dA"], ["dA"])
                actf(dA[:, c0:c0 + 64], dA[:, c0:c0 + 64], AF.Exp, ["dA"], ["dA"])
                tt(d3, d3, t3, MUL, ["dA", "dtT"], ["dA"])
                ci = 8
                bk, pk = nps()
                tr(bk[0:64, 0:32], dtT[:, c0:c0 + 64], ident_f[0:32, 0:32], ["dtT", "ident_f"], [pk])
                tr(bk[0:64, 32:64], dA[:, c0:c0 + 64], ident_f[0:32, 0:32], ["dA", "ident_f"], [pk])
                tr(bk[0:64, 64:96], acum[:, c0:c0 + 64], ident_f[0:32, 0:32], ["acum", "ident_f"], [pk])
                cp(tok[0:64, ci, :], bk[0:64, 0:96], [pk], ["tok"])
                dma("sp", ac_dram[:, :], acum[:, :], ["acum"], ["ac_dram"])

        def ssd_chan(sl, sk, off, chidx, dst, dkey, tiles, xkeys, main):
            ub = Z["uri"] % 2
            Z["uri"] += 1
            ur = Z["uraw"][ub]
            if main:
                bk, pk = nps()
                for kc in range(16):
                    mm(bk[:, 0:4], sl[:, kc, off:off + 128], xnT[:, kc, 0:4], kc == 0, kc == 15, [sk, xkeys[0]], [pk])
                cp(ur[:, 3:7], bk[:, 0:4], [pk], [("uraw", ub, -1)], eng="act")
            work = []
            for ti, (c0, n) in enumerate(tiles):
                bk, pk = nps()
                for kc in range(16):
                    mm(bk[:, 0:n], sl[:, kc, off:off + 128], xnT[:, kc, c0:c0 + n], kc == 0, kc == 15, [sk] + tile_keys(xkeys, c0, n), [pk])
                cp(ur[:, 3 + c0:3 + c0 + n], bk[:, 0:n], [pk], [("uraw", ub, ti)], eng="act")
                cs = Z["cti"] % 3
                Z["cti"] += 1
                work.append((False, ti, c0, n, cs))
            if main:
                c0, n = ST
                bk, pk = nps()
                for kc in range(16):
                    mm(bk[:, 0:n], sl[:, kc, off:off + 128], xnT[:, kc, c0:c0 + n], kc == 0, kc == 15, [sk] + tile_keys(xkeys, c0, n), [pk])
                ue = Z["uext"][ub]
                cp(ue[:, :, 0:3], Z["scT"][:, chidx, :].rearrange("p (s k) -> p s k", k=3), ["scT"], [("uext", ub)])
                cp(ue[:, :, 3:7], bk[:, 0:64].rearrange("p (s t) -> p s t", t=4), [pk], [("uext", ub)], eng="act")
                cs = Z["cti"] % 3
                Z["cti"] += 1
                work.append((True, -9, c0, 64, cs))
            dg, dgk = get_diag(("ssd", chidx, id(Z["uraw"])), [C_CSW + k * 24 + chidx for k in range(4)])
            bcol = csh[:, 96 + chidx:96 + chidx + 1]
            for (smp, ti, c0, n, cs) in work:
                ct = Z["ctmp"][cs]
                bk, pk = nps()
                if not smp:
                    for k in range(4):
                        mm(bk[:, 0:n], dg[:, k, :], ur[:, c0 + k:c0 + k + n], k == 0, k == 3, [("uraw", ub, ti), ("uraw", ub, ti - 1), dgk], [pk])
                else:
                    for k in range(4):
                        mm(bk[:, 0:64].rearrange("p (s t) -> p s t", t=4), dg[:, k, :], ue[:, :, k:k + 4], k == 0, k == 3, [("uext", ub), dgk], [pk])
                actf(dst[:, c0:c0 + n], bk[:, 0:n], AF.Tanh, [pk, "csh"], [dkey], bias=bcol, scale=0.5)
                actf(ct[:, 0:n], bk[:, 0:n], AF.Identity, [pk, "csh"], [("ctmp", cs)], bias=bcol, scale=0.5)
            for (smp, ti, c0, n, cs) in work:
                ct = Z["ctmp"][cs]
                stt(dst[:, c0:c0 + n], dst[:, c0:c0 + n], 1.0, ct[:, 0:n], ADD, MUL, [dkey, ("ctmp", cs)], [dkey])

        def tail_proj(sl, sk, ncol, ocol, xkeys):
            bk, pk = nps()
            for kc in range(16):
                mm(bk[0:67, 0:ncol], xnT[:, kc, 1025:1092], sl[:, kc, 0:ncol], kc == 0, kc == 15, [sk, xkeys[8]], [pk])
            tb = Z["tsi"] % 2
            Z["tsi"] += 1
            cp(Z["tailst"][tb][:, 0:ncol], bk[0:67, 0:ncol], [pk], [("tailst", tb)], eng="act")
            dma("sp", o_tail[:, ocol:ocol + ncol], Z["tailst"][tb][:, 0:ncol], [("tailst", tb)], ["out_tail_%d" % ocol])

        def tok_major(g, ci, c0, q, need_xdt):
            s = Z.get("cti2", 0) % 2
            Z["cti2"] = Z.get("cti2", 0) + 1
            gb = g % 2
            xs, Bs, tok = Z["xs"][gb], Z["Bs"][gb], Z["tok"]
            bk, pk = nps()
            for cc in range(4):
                mm(bk[0:q, cc * 128:(cc + 1) * 128], xs[:, cc, c0:c0 + q], ident_b, True, True, [("xs", gb, cc), "ident_b"], [pk])
            bk2, pk2 = nps()
            mm(bk2[0:q, 0:128], Bs[:, c0:c0 + q], ident_b, True, True, [("Bs", gb), "ident_b"], [pk2])
            xw, Bt = Z["xw"][s], Z["Bt"][s]
            x3 = bk[0:q, 0:512].rearrange("p (h d) -> p h d", d=64)
            wbc = tok[0:q, ci, 32 + 8 * g:32 + 8 * g + 8].unsqueeze(2).to_broadcast([q, 8, 64])
            tt(xw[0:q, :].rearrange("p (h d) -> p h d", d=64), x3, wbc, MUL, [pk, "tok"], [("xw", s)])
            xdt = None
            if need_xdt:
                xdt = Z["xdt"][s]
                dbc = tok[0:q, ci, 8 * g:8 * g + 8].unsqueeze(2).to_broadcast([q, 8, 64])
                tt(xdt[0:q, :].rearrange("p (h d) -> p h d", d=64), x3, dbc, MUL, [pk, "tok"], [("xdt", s)])
            cp(Bt[0:q, :], bk2[0:q, 0:128], [pk2], [("Bt", s)], eng="act")
            return xw, ("xw", s), Bt, ("Bt", s), xdt, ("xdt", s)

        def state_update(g, ci, q, xw, kxw, Bt, kBt, first):
            bk, pk = nps()
            mm(bk[:, 0:512], Bt[0:q, :], xw[0:q, :], True, True, [kBt, kxw], [pk])
            hg = hT[:, g * 512:(g + 1) * 512]
            if first:
                cp(hg, bk[:, 0:512], [pk], [("hT", g)])
            else:
                s = 0
                stp = Z["stmp"][s]
                dbc = Z["decbc"][:, ci, 8 * g:8 * g + 8].unsqueeze(2).to_broadcast([128, 8, 64])
                tt(stp.rearrange("p (h d) -> p h d", d=64), hg.rearrange("p (h d) -> p h d", d=64), dbc, MUL,
                   [("hT", g), "decbc"], [("stmp", s)])
                tt(hg, stp, bk[:, 0:512], ADD, [("stmp", s), pk], [("hT", g)])

        if stage >= 2:
            S.barrier()
            A.off = PH
            ssd_setup(False)
            ssd_dt(pre_keys, PT, PCH, False)
            for g in range(4 if stage >= 2.2 else 0):
                for sq in range(2):
                    sl, sk = load_w(w_in_cols(O3 + g * 512 + sq * 256, 256), 16, 256)
                    for j in range(2):
                        cc = sq * 2 + j
                        ssd_chan(sl, sk, j * 128, g * 4 + cc, Z["xs"][g % 2][:, cc, :], ("xs", g % 2, cc), PT, pre_keys, False)
                sl, sk = load_w(w_in_cols(O3 + 2048 + g * 128, 128), 16, 128)
                ssd_chan(sl, sk, 0, 16 + g, Z["Bs"][g % 2], ("Bs", g % 2), PT, pre_keys, False)
                for ci, (c0, q) in enumerate(PCH if stage >= 2.3 else []):
                    xw, kxw, Bt, kBt, _, _ = tok_major(g, ci, c0, q, False)
                    if stage < 2.4:
                        continue
                    state_update(g, ci, q, xw, kxw, Bt, kBt, ci == 0)
                    if ci == 0:
                        cp(hTm[:, g * 512:(g + 1) * 512], hT[:, g * 512:(g + 1) * 512], [("hT", g)], [("hTm", g)])
                if stage < 2.5:
                    continue
                hg, hm = hT[:, g * 512:(g + 1) * 512], hTm[:, g * 512:(g + 1) * 512]
                tt(hg, hg, hm, SUB, [("hT", g), ("hTm", g)], [("hT", g)])
                stt(hg, hg, cols[:, C_FLAG:C_FLAG + 1], hm, MUL, ADD, [("hT", g), ("hTm", g), "cols"], [("hT", g)])

        S.barrier()
        A.off = PH
        main_keys = norm_T(xm, NM, C_GMIX, xnT, "xnTm")
        lru_setup()
        hfin = A.alloc([128, 16], F32)
        hsfin = A.alloc([128, 16, 16], F32)
        lcT = A.alloc([128, 16, 48], F32)
        h0T = A.alloc([128, 16, 16], F32)
        tmp_lc = A.alloc([48, D], F32)
        tmp_h = A.alloc([16, D], F32)
        dma("sp", tmp_lc, st_lc, (), ["tmp_lc"])
        dma("sp", tmp_h, st_h, (), ["tmp_h"])
        for c in range(16):
            bk, pk = nps()
            tr(bk[:, 0:48], tmp_lc[:, c * 128:(c + 1) * 128], ident_f[0:48, 0:48], ["tmp_lc", "ident_f"], [pk])
            tr(bk[:, 64:80], tmp_h[:, c * 128:(c + 1) * 128], ident_f[0:16, 0:16], ["tmp_h", "ident_f"], [pk])
            cp(lcT[:, c, :], bk[:, 0:48], [pk], ["lcT"])
            cp(h0T[:, c, :], bk[:, 64:80], [pk], ["h0T"])
        uext = [A.alloc([128, 16, 8], BF16) for _ in range(2)]
        Z["tailst"] = [A.alloc([67, 256], F32) for _ in range(2)]
        Z["tsi"] = 0
        def main_B(h, sts):
            for ti in range(3):
                lru_sqrt(sts[ti], False)
            prev_hs, prev_key = None, None
            for ti, (c0, n) in enumerate(MT + [ST]):
                gsb = sts[ti]["T"]["gs"]
                gkey = sts[ti]["K"]("gs")
                if ti < 2:
                    init = h0col[:, h:h + 1] if ti == 0 else prev_hs[:, 511:512]
                    ikeys = ["h0col"] if ti == 0 else [prev_key]
                    hs, hk, uo, uk = lru_B(sts[ti], init, ikeys, gate_ps=gsb[:, 0:n], gate_key=gkey)
                    prev_hs, prev_key = hs, hk
                    if ti == 1:
                        cp(hfin[:, h:h + 1], hs[:, 511:512], [hk], ["hfin"])
                    dma("sp", u_dram[h, :, c0 - 4:c0 - 4 + n], uo[:, 0:n], [uk], [("u_dram", h)])
                else:
                    hs, hk, uo, uk = lru_B(sts[ti], None, ["h0T"], gate_ps=gsb[:, 0:64], gate_key=gkey, h0s=h0T[:, h, :])
                    cp(hsfin[:, h, :], hs[:, 0:64].rearrange("p (s t) -> p s t", t=4)[:, :, 3], [hk], ["hsfin"])
                    dma("sp", u_dram[h, :, 1024:1088], uo[:, 0:64], [uk], [("u_dram", h)])

        pend = None
        for hq in range(8 if (stage < 2 or stage >= 3) else 0):
            gs_, gk = load_w(w_in_cols(hq * 256, 256), 16, 256)
            xs_, xk = load_w(w_in_cols(O1 + hq * 256, 256), 16, 256)
            tail_proj(xs_, xk, 256, hq * 256, main_keys)
            for hh in range(2):
                h = hq * 2 + hh
                ub = h % 2
                ur = L["uraw"][ub]
                bk, pk = nps()
                for kc in range(16):
                    mm(bk[:, 0:4], xs_[:, kc, hh * 128:(hh + 1) * 128], xnT[:, kc, 0:4], kc == 0, kc == 15, [xk, main_keys[0]], [pk])
                cp(ur[:, 3:7], bk[:, 0:4], [pk], [("uraw", ub, -1)], eng="act")
                sts = []
                for ti, (c0, n) in enumerate(MT + [ST]):
                    sample = ti == 2
                    bk, pk = nps()
                    for kc in range(16):
                        mm(bk[:, 0:n], xs_[:, kc, hh * 128:(hh + 1) * 128], xnT[:, kc, c0:c0 + n], kc == 0, kc == 15,
                           [xk] + tile_keys(main_keys, c0, n), [pk])
                    bg, pg = nps()
                    for kc in range(16):
                        mm(bg[:, 0:n], gs_[:, kc, hh * 128:(hh + 1) * 128], xnT[:, kc, c0:c0 + n], kc == 0, kc == 15,
                           [gk] + tile_keys(main_keys, c0, n), [pg])
                    if not sample:
                        cp(ur[:, 3 + c0:3 + c0 + n], bk[:, 0:n], [pk], [("uraw", ub, ti)], eng="act")
                        st_ = lru_A(h, n, lambda k, ur=ur, c0=c0, n=n: ur[:, c0 + k:c0 + k + n], [("uraw", ub, ti), ("uraw", ub, ti - 1)])
                    else:
                        ue = uext[ub]
                        cp(ue[:, :, 0:3], lcT[:, h, :].rearrange("p (s k) -> p s k", k=3), ["lcT"], [("uext", ub)])
                        cp(ue[:, :, 3:7], bk[:, 0:64].rearrange("p (s t) -> p s t", t=4), [pk], [("uext", ub)], eng="act")
                        st_ = lru_A(h, 64, lambda k, ue=ue: ue[:, :, k:k + 4], [("uext", ub)], sample=True)
                    gsb = st_["T"]["gs"]
                    cp(gsb[:, 0:n], bg[:, 0:n], [pg], [st_["K"]("gs")], eng="act")
                    sts.append(st_)
                run_stages(sts)
                if pend is not None:
                    main_B(*pend)
                pend = (h, sts)
        main_B(*pend)
        dma_nc("sp", o_plh.rearrange("(h p) -> p h", p=128), hfin, ["hfin"], ["out_plh"])
        slh = tmp_h
        for c in range(16):
            bk, pk = nps()
            tr(bk[0:16, 0:128], hsfin[:, c, :], ident_f, ["hsfin", "ident_f"], [pk])
            cp(slh[:, c * 128:(c + 1) * 128], bk[0:16, 0:128], [pk], ["tmp_h"])
        dma("sp", o_slh, slh, ["tmp_h"], ["out_slh"])

        if stage >= 3:
            S.barrier()
            A.off = PH
            ssd_setup(True)
            tmp_sc = hTm[0:48, 0:1536]
            for hf in range(2):
                dma("sp", tmp_sc, st_sc[:, hf * 1536:(hf + 1) * 1536], (), ["scr8k"])
                for c in range(12):
                    bk, pk = nps()
                    tr(bk[:, 0:48], tmp_sc[:, c * 128:(c + 1) * 128], ident_f[0:48, 0:48], ["scr8k", "ident_f"], [pk])
                    cp(Z["scT"][:, hf * 12 + c, :], bk[:, 0:48], [pk], ["scT"])
            ssd_dt(main_keys, MT + [ST], MCH, True)
            def proj_units(g):
                gb = g % 2
                xs, Bs, Cs, zs = Z["xs"][gb], Z["Bs"][gb], Z["Cs"][gb], Z["zs"][gb]
                units = []

                def uz(sq):
                    sl, sk = load_w(w_in_cols(O2 + g * 512 + sq * 256, 256), 16, 256)
                    for j in range(2):
                        cc = sq * 2 + j
                        for (c0, n) in MT + [ST]:
                            bk, pk = nps()
                            for kc in range(16):
                                mm(bk[:, 0:n], sl[:, kc, j * 128:(j + 1) * 128], xnT[:, kc, c0:c0 + n], kc == 0, kc == 15,
                                   [sk] + tile_keys(main_keys, c0, n), [pk])
                            zv = zs[:, cc, c0 - 4:c0 - 4 + n]
                            actf(zv, bk[:, 0:n], AF.Tanh, [pk], [("zs", gb, cc)], scale=0.5)
                            stt(zv, zv, 1.0, bk[:, 0:n], ADD, MUL, [pk, ("zs", gb, cc)], [("zs", gb, cc)])

                def ux(sq):
                    sl, sk = load_w(w_in_cols(O3 + g * 512 + sq * 256, 256), 16, 256)
                    tail_proj(sl, sk, 256, 2048 + g * 512 + sq * 256, main_keys)
                    for j in range(2):
                        cc = sq * 2 + j
                        ssd_chan(sl, sk, j * 128, g * 4 + cc, xs[:, cc, :], ("xs", gb, cc), MT, main_keys, True)

                def uB():
                    sl, sk = load_w(w_in_cols(O3 + 2048 + g * 128, 128), 16, 128)
                    tail_proj(sl, sk, 128, 2048 + 2048 + g * 128, main_keys)
                    ssd_chan(sl, sk, 0, 16 + g, Bs, ("Bs", gb), MT, main_keys, True)

                def uC():
                    sl, sk = load_w(w_in_cols(O3 + 2560 + g * 128, 128), 16, 128)
                    tail_proj(sl, sk, 128, 2048 + 2560 + g * 128, main_keys)
                    ssd_chan(sl, sk, 0, 20 + g, Cs, ("Cs", gb), MT, main_keys, True)

                def ustash():
                    cp(Z["xs_s"][:, 4 * g:4 * g + 4, :], xs[:, :, ST[0]:ST[0] + 64], [("xs", gb, c) for c in range(4)], ["xs_s"])
                    cp(Z["zs_s"][:, 4 * g:4 * g + 4, :], zs[:, :, 1024:1088], [("zs", gb, c) for c in range(4)], ["zs_s"])
                    cp(Z["Bs_s"][:, g, :], Bs[:, ST[0]:ST[0] + 64], [("Bs", gb)], ["Bs_s"])
                    cp(Z["Cs_s"][:, g, :], Cs[:, ST[0]:ST[0] + 64], [("Cs", gb)], ["Cs_s"])
                return [lambda: uB(), lambda: uC(), lambda: ux(0), lambda: ux(1), lambda: uz(0), lambda: uz(1), ustash]

            for u_ in proj_units(0):
                u_()
            for g in range(4):
                gb = g % 2
                xs, Bs, Cs, zs, hTb, tok = Z["xs"][gb], Z["Bs"][gb], Z["Cs"][gb], Z["zs"][gb], Z["hTb"], Z["tok"]
                nxt = proj_units(g + 1) if g < 3 else []
                hg = hT[:, g * 512:(g + 1) * 512]
                hbg = hTb[:, g * 512:(g + 1) * 512]
                cp(hbg, hg, [("hT", g)], [("hTb", g)])
                def pre(ci):
                    c0, q = MCH[ci]
                    s = ci % 2
                    Rb, Eo, Lm, Csc, cbT = (Z[k][s] for k in ("Rb", "Eo", "Lm", "Csc", "cbT"))
                    dma("sp", Rb, ac_dram[8 * g:8 * g + 8, c0:c0 + 128].partition_broadcast(128), ["ac_dram"], [("Rb", s)])
                    actf(Eo, Rb, AF.Exp, [("Rb", s)], [("Eo", s)])
                    tt(Rb, Rb, negmask.unsqueeze(1).to_broadcast([128, 8, 128]), ADD, [("Rb", s), "negmask"], [("Rb", s)])
                    tt(Rb, Rb, tok[:, ci, 64 + 8 * g:64 + 8 * g + 8].unsqueeze(2).to_broadcast([128, 8, 128]), SUB,
                       [("Rb", s), "tok"], [("Rb", s)])
                    actf(Lm, Rb, AF.Exp, [("Rb", s)], [("Lm", s)])
                    bk, pk = nps()
                    mm(bk[:, 0:128], Bs[:, c0:c0 + 128], Cs[:, c0:c0 + 128], True, True, [("Bs", gb), ("Cs", gb)], [pk])
                    cp(cbT, bk[:, 0:128], [pk], [("cbT", s)], eng="act")
                    tt(Lm, Lm, cbT.unsqueeze(1).to_broadcast([128, 8, 128]), MUL, [("Lm", s), ("cbT", s)], [("Lm", s)])
                    tt(Csc, Eo, Cs[:, c0:c0 + 128].unsqueeze(1).to_broadcast([128, 8, 128]), MUL, [("Eo", s), ("Cs", gb)], [("Csc", s)])
                    return tok_major(g, ci, c0, 128, True)

                def post(ci, tm):
                    c0, q = MCH[ci]
                    s = ci % 2
                    Lm, Csc, yt, uo = (Z[k][s] for k in ("Lm", "Csc", "yt", "uo"))
                    xw, kxw, Bt, kBt, xdt, kxdt = tm
                    by, py = nps()
                    for hp in range(4):
                        for e2 in range(2):
                            hh = hp * 2 + e2
                            o = by[64 * e2:64 * e2 + 64, hp * 128:(hp + 1) * 128]
                            mm(o, xdt[:, hh * 64:(hh + 1) * 64], Lm[:, hh, :], True, False, [kxdt, ("Lm", s)], [py])
                            mm(o, hbg[:, hh * 64:(hh + 1) * 64], Csc[:, hh, :], False, True, [("hTb", g), ("Csc", s)], [py])
                    dsk = cols[:, C_DSK + 4 * g:C_DSK + 4 * g + 4].unsqueeze(2).to_broadcast([128, 4, 128])
                    tt(yt, xs[:, :, c0:c0 + 128], dsk, MUL, [("xs", gb, c) for c in range(4)] + ["cols"], [("yt", s)])
                    tt(yt, yt, by[:, 0:512].rearrange("p (c t) -> p c t", c=4), ADD, [("yt", s), py], [("yt", s)])
                    stt(uo, yt, 0.5, zs[:, :, c0 - 4:c0 - 4 + 128], MUL, MUL, [("yt", s)] + [("zs", gb, c) for c in range(4)], [("uo", s)])
                    dma("sp", u_dram[16 + 4 * g:16 + 4 * g + 4, :, c0 - 4:c0 - 4 + 128].rearrange("c p t -> p c t"), uo,
                        [("uo", s)], [("u_dram", 16 + g)])
                    state_update(g, ci, 128, xw, kxw, Bt, kBt, False)
                    cp(hbg, hg, [("hT", g)], [("hTb", g)], eng="act")

                tm_next = pre(0)
                for ci in range(8):
                    tm_cur = tm_next
                    if ci + 1 < 8:
                        tm_next = pre(ci + 1)
                    post(ci, tm_cur)
                    if ci < len(nxt):
                        nxt[ci]()
                for u_ in nxt[8:]:
                    u_()
            pst_o = hTm.rearrange("p (c n) -> p c n", c=16)
            for c in range(16):
                bk, pk = nps()
                tr(bk[:, 0:128], hT[:, c * 128:(c + 1) * 128], ident_f, [("hT", c // 4), "ident_f"], [pk])
                cp(pst_o[:, c, :], bk[:, 0:128], [pk], ["scr8k"])
            dma("sp", o_pssd.rearrange("(c p) n -> p c n", p=128), pst_o, ["scr8k"], ["out_pssd"])

        if stage >= 5:
            S.barrier()
            A.off = Z["mark_s"]
            tok, acum = Z["tok"], Z["acum"]
            xs_s, zs_s, Bs_s, Cs_s = Z["xs_s"], Z["zs_s"], Z["Bs_s"], Z["Cs_s"]
            c0 = ST[0]
            SEL = A.alloc([32, 16, 128], F32)
            S.pool(lambda e: e.memset(SEL, 1.0), (), ["SEL"])
            SEL4 = SEL.rearrange("p c (a b) -> p c a b", a=2)
            S.pool(lambda e: e.affine_select(out=SEL4, in_=SEL4, pattern=[[-2, 16], [-1, 2], [0, 64]], compare_op=ALU.is_equal,
                                             fill=0.0, base=0, channel_multiplier=1), ["SEL"], ["SEL"])
            seqmask = A.alloc([64, 16], F32)
            S.pool(lambda e: e.memset(seqmask, 1.0), (), ["seqmask"])
            S.pool(lambda e: e.affine_select(out=seqmask, in_=seqmask, pattern=[[-4, 16]], compare_op=ALU.is_ge,
                                             fill=0.0, base=0, channel_multiplier=1), ["seqmask"], ["seqmask"])
            S.pool(lambda e: e.affine_select(out=seqmask, in_=seqmask, pattern=[[4, 16]], compare_op=ALU.is_ge,
                                             fill=0.0, base=3, channel_multiplier=-1), ["seqmask"], ["seqmask"])
            eac = A.alloc([32, 64], F32)
            actf(eac, acum[:, c0:c0 + 64], AF.Exp, ["acum"], ["eac"])
            Eb = A.alloc([128, 16, 64], F32)
            for ch in range(16):
                bk, pk = nps()
                mm(bk[:, 0:64], SEL[:, ch, :], eac, True, True, ["SEL", "eac"], [pk])
                cp(Eb[:, ch, :], bk[:, 0:64], [pk], ["Eb"], eng="act")
            Rb_s = [A.alloc([64, 8, 64], F32) for _ in range(2)]
            L_s = [A.alloc([64, 8, 64], BF16) for _ in range(2)]
            cbT_s = [A.alloc([64, 64], BF16) for _ in range(2)]
            xdt_s = A.alloc([64, 2048], BF16)
            xw_s = A.alloc([64, 2048], BF16)
            Bt_s = A.alloc([64, 4, 128], BF16)
            ydiag = A.alloc([128, 16, 64], F32)
            for g in range(4):
                s = g % 2
                dma("sp", Rb_s[s], ac_dram[8 * g:8 * g + 8, c0:c0 + 64].partition_broadcast(64), ["ac_dram"], [("Rb_s", s)])
                tt(Rb_s[s], Rb_s[s], negmask_s.unsqueeze(1).to_broadcast([64, 8, 64]), ADD, [("Rb_s", s), "negmask_s"], [("Rb_s", s)])
                tt(Rb_s[s], Rb_s[s], tok[0:64, 8, 64 + 8 * g:64 + 8 * g + 8].unsqueeze(2).to_broadcast([64, 8, 64]), SUB,
                   [("Rb_s", s), "tok"], [("Rb_s", s)])
                actf(L_s[s], Rb_s[s], AF.Exp, [("Rb_s", s)], [("L_s", s)])
                bk, pk = nps()
                mm(bk[0:64, 0:64], Bs_s[:, g, :], Cs_s[:, g, :], True, True, ["Bs_s", "Cs_s"], [pk])
                cp(cbT_s[s], bk[0:64, 0:64], [pk], [("cbT_s", s)], eng="act")
                tt(L_s[s], L_s[s], cbT_s[s].unsqueeze(1).to_broadcast([64, 8, 64]), MUL, [("L_s", s), ("cbT_s", s)], [("L_s", s)])
                bk, pk = nps()
                for cc in range(4):
                    mm(bk[0:64, cc * 128:(cc + 1) * 128], xs_s[:, 4 * g + cc, :], ident_b, True, True, ["xs_s", "ident_b"], [pk])
                x3 = bk[0:64, 0:512].rearrange("p (h d) -> p h d", d=64)
                tt(xdt_s[:, g * 512:(g + 1) * 512].rearrange("p (h d) -> p h d", d=64), x3,
                   tok[0:64, 8, 8 * g:8 * g + 8].unsqueeze(2).to_broadcast([64, 8, 64]), MUL, [pk, "tok"], [("xdt_s", g)])
                tt(xw_s[:, g * 512:(g + 1) * 512].rearrange("p (h d) -> p h d", d=64), x3,
                   tok[0:64, 8, 32 + 8 * g:32 + 8 * g + 8].unsqueeze(2).to_broadcast([64, 8, 64]), MUL, [pk, "tok"], [("xw_s", g)])
                bk2, pk2 = nps()
                mm(bk2[0:64, 0:128], Bs_s[:, g, :], ident_b, True, True, ["Bs_s", "ident_b"], [pk2])
                cp(Bt_s[:, g, :], bk2[0:64, 0:128], [pk2], ["Bt_s"], eng="act")
                by, py = nps()
                for hp in range(4):
                    for e2 in range(2):
                        hh = hp * 2 + e2
                        mm(by[64 * e2:64 * e2 + 64, hp * 64:(hp + 1) * 64], xdt_s[:, g * 512 + hh * 64:g * 512 + (hh + 1) * 64], L_s[s][:, hh, :],
                           True, True, [("xdt_s", g), ("L_s", s)], [py])
                cp(ydiag[:, 4 * g:4 * g + 4, :], by[:, 0:256].rearrange("p (c t) -> p c t", c=4), [py], ["ydiag"])
            yb = []
            while len(yb) < 2:
                b = psi[0] % 8
                psi[0] += 1
                if b not in reserved:
                    reserved.add(b)
                    yb.append(b)
            NBS = 4
            stb = [A.alloc([128, 16, 128], F32) for _ in range(NBS)]
            stT = [A.alloc([128, 2048], BF16) for _ in range(NBS)]
            Bm = [A.alloc([64, 512], BF16) for _ in range(NBS)]
            Bt_flat = Bt_s.rearrange("p g n -> p (g n)")

            def ld_state(q_):
                dma("sp", stb[q_ % NBS], st_ssd[q_].rearrange("(c p) n -> p c n", p=128), (), [("stb", q_ % NBS)])
            for q_ in range(NBS - 1):
                ld_state(q_)
            for sq_ in range(16):
                s = sq_ % NBS
                ks = ("stb", s)
                if sq_ + NBS - 1 < 16:
                    ld_state(sq_ + NBS - 1)
                for q4 in range(4):
                    bk, pk = nps()
                    for j in range(4):
                        tr(bk[:, j * 128:(j + 1) * 128], stb[s][:, 4 * q4 + j, :], ident_f, [ks, "ident_f"], [pk])
                    cp(stT[s][:, q4 * 512:(q4 + 1) * 512], bk[:, 0:512], [pk], [("stT", s, q4)], eng="act")
                ybk = banks[yb[sq_ // 8]]
                ykey = ("ps", yb[sq_ // 8])
                for ch in range(16):
                    off = (sq_ % 8) * 64 + ch * 4
                    mm(ybk[:, off:off + 4], stT[s][:, ch * 128:(ch + 1) * 128], Cs_s[:, ch // 4, 4 * sq_:4 * sq_ + 4], True, True,
                       [("stT", s, ch // 4), "Cs_s"], [ykey])
                ts(Bm[s], Bt_flat, seqmask[:, sq_:sq_ + 1], None, MUL, None, ["Bt_s", "seqmask"], [("Bm", s)])
                for q4 in range(4):
                    bk, pk = nps()
                    for j in range(4):
                        ch = 4 * q4 + j
                        mm(bk[:, j * 128:(j + 1) * 128], xw_s[:, ch * 128:(ch + 1) * 128], Bm[s][:, (ch // 4) * 128:(ch // 4 + 1) * 128],
                           True, True, [("xw_s", ch // 4), ("Bm", s)], [pk])
                    sv = stb[s][:, 4 * q4:4 * q4 + 4, :]
                    tt(sv, sv, Eb[:, 4 * q4:4 * q4 + 4, 4 * sq_ + 3:4 * sq_ + 4].to_broadcast([128, 4, 128]), MUL, [ks, "Eb"], [ks])
                    tt(sv, sv, bk[:, 0:512].rearrange("p (c n) -> p c n", c=4), ADD, [ks, pk], [ks])
                dma("sp", o_sssd[sq_].rearrange("(c p) n -> p c n", p=128), stb[s], [ks], ["out_sssd%d" % sq_])
            yo = A.alloc([128, 16, 16, 4], F32)
            for hb in range(2):
                cp(yo[:, 8 * hb:8 * hb + 8, :, :].rearrange("p s c t -> p (s c t)"), banks[yb[hb]][:, 0:512], [("ps", yb[hb])], ["yo"])
            reserved.clear()
            ysm = A.alloc([128, 16, 64], F32)
            ysm4 = ysm.rearrange("p c (s t) -> p c s t", t=4)
            tt(ysm4, yo.rearrange("p s c t -> p c s t"), Eb.rearrange("p c (s t) -> p c s t", t=4), MUL, ["yo", "Eb"], ["ysm"])
            tt(ysm, ysm, ydiag, ADD, ["ysm", "ydiag"], ["ysm"])
            ytmp = A.alloc([128, 16, 64], F32)
            tt(ytmp, xs_s, cols[:, C_DSK:C_DSK + 16].unsqueeze(2).to_broadcast([128, 16, 64]), MUL, ["xs_s", "cols"], ["ytmp"])
            tt(ysm, ysm, ytmp, ADD, ["ysm", "ytmp"], ["ysm"])
            uo_s = A.alloc([128, 16, 64], BF16)
            stt(uo_s, ysm, 0.5, zs_s, MUL, MUL, ["ysm", "zs_s"], ["uo_s"])
            dma("sp", u_dram[16:32, :, 1024:1088].rearrange("c p t -> p c t"), uo_s, ["uo_s"], [("u_dram", 99)])

        HALVES = [[(0, 128), (128, 128), (256, 128), (384, 128)], [(512, 128), (640, 128), (768, 128), (896, 128), (1024, 64)]]
        PH_OUT = PH - NM * 16
        if stage >= 6:
            S.barrier()
            A.off = PH_OUT
            ubuf = A.alloc([128, 32, NU], BF16)
            for c8 in range(4):
                dma("sp", ubuf[:, c8 * 8:(c8 + 1) * 8, :], u_dram[c8 * 8:(c8 + 1) * 8].rearrange("c p t -> p c t"), (),
                    [("ubuf", c) for c in range(c8 * 8, c8 * 8 + 8)])
            sqb = [A.alloc([128, NU], BF16) for _ in range(2)]
            rstd = A.alloc([128, NU], F32)
            groups = [(list(range(16)), 2048, C_GLRU)] + [([16 + 4 * g + j for j in range(4)], 512, C_GSSD + 4 * g) for g in range(4)]
            NT3 = [(0, 512), (512, 512), (1024, 64)]
            for gi, (chs, N, gc) in enumerate(groups):
                b3 = [nps() for _ in range(3)]
                for i, c in enumerate(chs):
                    s = i % 2
                    tt(sqb[s], ubuf[:, c, :], ubuf[:, c, :], MUL, [("ubuf", c)], [("sqb", s)])
                    for (bk, pk), (c0, n) in zip(b3, NT3):
                        mm(bk[:, 0:n], ones_b, sqb[s][:, c0:c0 + n], i == 0, i == len(chs) - 1, ["ones_b", ("sqb", s)], [pk])
                for (bk, pk), (c0, n) in zip(b3, NT3):
                    actf(rstd[:, c0:c0 + n], bk[:, 0:n], AF.Sqrt, [pk, "epsc"], ["rstd"], bias=epsc[:, 0:1], scale=1.0 / N)
                S.dve(lambda e: e.reciprocal(out=rstd, in_=rstd), ["rstd"], ["rstd"])
                for j, c in enumerate(chs):
                    stt(ubuf[:, c, :], ubuf[:, c, :], cols[:, gc + j:gc + j + 1], rstd, MUL, MUL, [("ubuf", c), "cols", "rstd"], [("ubuf", c)])
            xres = [A.alloc([128, 512], F32) for _ in range(3)]
            x1t = [A.alloc([128, 512], F32) for _ in range(3)]
            cnt = 0
            for hf in range(2):
                tiles = HALVES[hf]
                for fb in range(4):
                    bks = [nps() for _ in tiles]
                    for kq in range(4):
                        sl, sk = load_w(w_out[kq * 1024:(kq + 1) * 1024, fb * 512:(fb + 1) * 512].rearrange("(k p) n -> p k n", p=128), 8, 512)
                        for kk in range(8):
                            kc = kq * 8 + kk
                            for ti, (t0, nt) in enumerate(tiles):
                                mm(bks[ti][0][0:nt, :], ubuf[:, kc, t0:t0 + nt], sl[:, kk, :], kc == 0, kc == 31, [sk, ("ubuf", kc)], [bks[ti][1]])
                    for ti, (t0, nt) in enumerate(tiles):
                        i = cnt % 3
                        cnt += 1
                        dma("sp", xres[i][0:nt], xm[4 + t0:4 + t0 + nt, fb * 512:(fb + 1) * 512], (), [("xres", i)])
                        tt(x1t[i][0:nt], bks[ti][0][0:nt, :], xres[i][0:nt], ADD, [bks[ti][1], ("xres", i)], [("x1t", i)])
                        dma("sp", x1_dram[t0:t0 + nt, fb * 512:(fb + 1) * 512], x1t[i][0:nt], [("x1t", i)], [("x1d", hf, ti)])

        if stage >= 7:
            S.barrier()
            A.off = PH_OUT
            gfb = A.alloc([128, D], F32)
            dma("sp", gfb, g_final.partition_broadcast(128), (), ["gfb"])
            x1s = A.alloc([128, 5, D], F32)
            hpT = A.alloc([128, 16, 576], BF16)
            hbuf = A.alloc([128, 64, 576], BF16)
            rl = [A.alloc([128, 512], F32) for _ in range(2)]
            fj = A.alloc([128, D], BF16)
            fs = A.alloc([128, 2], F32)
            nb_mark = A.off
            ri = 0
            for hf in range(2):
                tiles = HALVES[hf]
                T0 = tiles[0][0]
                ntok = sum(nt for _, nt in tiles)
                for ti, (t0, nt) in enumerate(tiles):
                    dma("sp", x1s[0:nt, ti, :], x1_dram[t0:t0 + nt, :], [("x1d", hf, ti)], [("x1s", ti)])
                A.off = nb_mark
                hkeys = norm_T(None, ntok, C_GMLP, hpT, "hpT", pre=[(x1s[:, ti, :], ("x1s", ti)) for ti in range(len(tiles))])
                Ntiles = [(0, 512)] + ([(512, 64)] if hf == 1 else [])
                for jq in range(32):
                    sl, sk = load_w(w_up[:, jq * 256:(jq + 1) * 256].rearrange("(k p) n -> p k n", p=128), 16, 256)
                    for jj in range(2):
                        j = 2 * jq + jj
                        for (c0, n) in Ntiles:
                            bk, pk = nps()
                            for kc in range(16):
                                mm(bk[:, 0:n], sl[:, kc, jj * 128:(jj + 1) * 128], hpT[:, kc, c0:c0 + n], kc == 0, kc == 15,
                                   [sk] + tile_keys(hkeys, c0, n), [pk])
                            r = rl[ri % 2]
                            rk = ("rl", ri % 2)
                            ri += 1
                            actf(r[:, 0:n], bk[:, 0:n], AF.Relu, [pk], [rk])
                            tt(hbuf[:, j, c0:c0 + n], r[:, 0:n], r[:, 0:n], MUL, [rk], [("hbuf", j)])
                for fb in range(4):
                    bks = [nps() for _ in tiles]
                    for jq in range(8):
                        sl, sk = load_w(w_down[jq * 1024:(jq + 1) * 1024, fb * 512:(fb + 1) * 512].rearrange("(k p) n -> p k n", p=128), 8, 512)
                        for jj in range(8):
                            j = jq * 8 + jj
                            for ti, (t0, nt) in enumerate(tiles):
                                mm(bks[ti][0][0:nt, :], hbuf[:, j, t0 - T0:t0 - T0 + nt], sl[:, jj, :], j == 0, j == 63, [sk, ("hbuf", j)], [bks[ti][1]])
                    for ti, (t0, nt) in enumerate(tiles):
                        xv = x1s[0:nt, ti, fb * 512:(fb + 1) * 512]
                        tt(xv, bks[ti][0][0:nt, :], xv, ADD, [bks[ti][1], ("x1s", ti)], [("x1s", ti)])
                for ti, (t0, nt) in enumerate(tiles):
                    xv = x1s[0:nt, ti, :]
                    mset(fs[0:nt], 0.0, ["fs"])
                    actf(fj[0:nt], xv, AF.Square, [("x1s", ti), "fs"], ["fj", "fs"], accum=fs[0:nt, 0:1])
                    actf(fs[0:nt, 1:2], fs[0:nt, 0:1], AF.Sqrt, ["fs", "epsc"], ["fs"], bias=epsc[0:nt, 0:1], scale=1.0 / D)
                    S.dve(lambda e, nt=nt: e.reciprocal(out=fs[0:nt, 1:2], in_=fs[0:nt, 1:2]), ["fs"], ["fs"])
                    stt(xv, xv, fs[0:nt, 1:2], gfb[0:nt], MUL, MUL, [("x1s", ti), "fs", "gfb"], [("x1s", ti)])
                    dma("sp", y_out[t0:t0 + nt, :], xv, [("x1s", ti)], ["out_y_%d_%d" % (hf, ti)])

        print("arena high-water marks", ARENA_HW, "of", NAR)
        outs = [op.writes[0] for op in S.ops if op.dma and op.writes and isinstance(op.writes[0], str) and op.writes[0].startswith("out_")]
        S.add("sp", None, reads=outs)
        S.emit()
    return nc


def _colize(v):
    return np.ascontiguousarray(np.asarray(v, np.float32).reshape(-1, 128).T)


_NC_CACHE = {}


def kernel(x_prompt, x_sample, state_lru_h, state_lru_conv, state_ssd, state_ssd_conv, meta_tokens,
           g_mix, w_in, conv_lru_w, conv_lru_b, lru_wa, lru_ba, lru_wx, lru_bx, lru_lambda, g_lru_out,
           conv_ssd_w, conv_ssd_b, dt_bias, a_log, d_skip, g_ssd_out, w_out, g_mlp, w_up, w_down, g_final,
           _stage=99):
    f = lambda a: np.ascontiguousarray(np.asarray(a, dtype=np.float32))
    x_prompt, x_sample, meta_tokens = f(x_prompt), f(x_sample), f(meta_tokens)
    cols = np.zeros((128, NCOLS), np.float32)
    cols[:, C_GMIX:C_GMIX + 16] = _colize(g_mix[0])
    cols[:, C_GMLP:C_GMLP + 16] = _colize(g_mlp[0])
    for k in range(4):
        cols[:, C_CLW + k * 16:C_CLW + (k + 1) * 16] = _colize(conv_lru_w[0, k])
        cols[:, C_CSW + k * 24:C_CSW + (k + 1) * 24] = _colize(conv_ssd_w[0, k])
    cols[:, C_CLB:C_CLB + 16] = _colize(conv_lru_b[0])
    cols[:, C_BA:C_BA + 16] = _colize(lru_ba[0])
    cols[:, C_BX:C_BX + 16] = _colize(lru_bx[0])
    cols[:, C_LAM:C_LAM + 16] = _colize(lru_lambda[0])
    cols[:, C_GLRU:C_GLRU + 16] = _colize(g_lru_out[0])
    cols[:, C_CSB:C_CSB + 24] = _colize(conv_ssd_b[0])
    cols[:, C_GSSD:C_GSSD + 16] = _colize(g_ssd_out[0])
    cols[:, C_DSK:C_DSK + 16] = _colize(np.repeat(np.asarray(d_skip[0], np.float32), 64))
    cols[0:32, C_DTB] = np.asarray(dt_bias[0], np.float32)
    cols[0:32, C_ALOG] = np.asarray(a_log[0], np.float32)
    shared = dict(w_in=f(w_in[0]), lru_wa=f(lru_wa[0]), lru_wx=f(lru_wx[0]), w_out=f(w_out[0]), w_up=f(w_up[0]),
                  w_down=f(w_down[0]), g_final=f(g_final))
    in_maps = []
    for c in range(8):
        b, half = c // 2, c % 2
        cc = cols.copy()
        cc[:, C_FLAG] = float(half)
        xp = np.concatenate([meta_tokens, x_prompt[b, 0:1024]], axis=0)
        halo = meta_tokens[13:16] if half == 0 else x_prompt[b, 1021:1024]
        xm = np.concatenate([np.zeros((1, D), np.float32), halo, x_prompt[b, half * 1024:(half + 1) * 1024], x_sample[16 * c:16 * c + 16].reshape(64, D)], axis=0)
        m = dict(shared)
        m.update(xp=np.ascontiguousarray(xp), xm=np.ascontiguousarray(xm), cols=cc,
                 st_lru_h=f(state_lru_h[0, 16 * c:16 * c + 16]),
                 st_lru_conv=f(state_lru_conv[0, 16 * c:16 * c + 16]).reshape(48, D),
                 st_ssd=f(state_ssd[0, 16 * c:16 * c + 16]).reshape(16, 2048, 128),
                 st_ssd_conv=f(state_ssd_conv[0, 16 * c:16 * c + 16]).reshape(48, 3072))
        in_maps.append(m)
    if _stage not in _NC_CACHE:
        _NC_CACHE[_stage] = build(_stage)
    nc = _NC_CACHE[_stage]
    res = run_bass_kernel_spmd(nc, in_maps, core_ids=list(range(8)))
    R = res.results
    y_prompt = np.zeros((4, 2048, D), np.float32)
    y_sample = np.zeros((128, 4, D), np.float32)
    p_lru_h = np.zeros((1, 4, D), np.float32)
    p_lru_conv = np.zeros((1, 4, 3, D), np.float32)
    p_ssd = np.zeros((1, 4, 32, 64, 128), np.float32)
    p_ssd_conv = np.zeros((1, 4, 3, 3072), np.float32)
    s_lru_h = np.zeros((1, 128, D), np.float32)
    s_lru_conv = np.zeros((1, 128, 3, D), np.float32)
    s_ssd = np.zeros((1, 128, 32, 64, 128), np.float32)
    s_ssd_conv = np.zeros((1, 128, 3, 3072), np.float32)
    for c in range(8):
        b, half = c // 2, c % 2
        r = R[c]
        y_prompt[b, half * 1024:(half + 1) * 1024] = r["y_out"][0:1024]
        y_sample[16 * c:16 * c + 16] = r["y_out"][1024:1088].reshape(16, 4, D)
        tail = r["tail"]
        ts_ = tail[3:67].reshape(16, 4, 5120)[:, 1:4]
        s_lru_conv[0, 16 * c:16 * c + 16] = ts_[:, :, 0:2048]
        s_ssd_conv[0, 16 * c:16 * c + 16] = ts_[:, :, 2048:5120]
        s_lru_h[0, 16 * c:16 * c + 16] = r["s_lru_h"]
        s_ssd[0, 16 * c:16 * c + 16] = r["s_ssd"].reshape(16, 32, 64, 128)
        if half == 1:
            p_lru_h[0, b] = r["p_lru_h"]
            p_lru_conv[0, b] = tail[0:3, 0:2048]
            p_ssd_conv[0, b] = tail[0:3, 2048:5120]
            p_ssd[0, b] = r["p_ssd"].reshape(32, 64, 128)
    return (y_prompt, y_sample, p_lru_h, p_lru_conv, p_ssd, p_ssd_conv, s_lru_h, s_lru_conv, s_ssd, s_ssd_conv)
```
